# Optimizing a Trainium2 kernel written in Bass

```python
import jax, jax.numpy as jnp
from jax import lax
import numpy as np

D_MODEL = 2048
BATCH = 2
SEQ = 8192
DEPTH = 4

N_META = 16
D_MIX = D_MODEL
GLA_WIDTH = D_MIX // 2
CONV_WIDTH = D_MIX - GLA_WIDTH
GLA_HEADS = 4
GLA_HEAD_V = GLA_WIDTH // GLA_HEADS
GLA_HEAD_K = GLA_HEAD_V // 2
GLA_KEY = GLA_HEADS * GLA_HEAD_K
GATE_RANK = 16
GATE_TAU = 16.0
CHUNK = 64
CONV_K = 3
EPS = 1e-6

PROJ_SIZES = (GLA_KEY, GLA_KEY, GLA_WIDTH, GLA_WIDTH, GATE_RANK,
              CONV_WIDTH, CONV_WIDTH, CONV_WIDTH, CONV_WIDTH)
D_PROJ = sum(PROJ_SIZES)
SPLIT_POINTS = tuple(int(s) for s in np.cumsum(PROJ_SIZES)[:-1])

kernel_name = 'hymba_gla_shortconv_hybrid'


def rmsnorm(x, gain):
    xf = x.astype(jnp.float32)
    y = xf * lax.rsqrt(jnp.mean(xf * xf, axis=-1, keepdims=True) + EPS)
    return (y * gain.astype(jnp.float32)).astype(x.dtype)


def gla_chunked(q, k, v, log_a):
    bsz, L, H, DK = q.shape
    DV = v.shape[-1]
    pad = (-L) % CHUNK
    padf = lambda t: jnp.pad(t, ((0, 0), (pad, 0), (0, 0), (0, 0)))
    q, k, v, log_a = padf(q), padf(k), padf(v), padf(log_a)
    n_chunks = (L + pad) // CHUNK
    rs = lambda t: t.reshape(bsz, n_chunks, CHUNK, H, t.shape[-1])
    q, k, v, log_a = rs(q), rs(k), rs(v), rs(log_a)
    b = jnp.cumsum(log_a, axis=2)
    b_mid = b[:, :, CHUNK // 2 - 1:CHUNK // 2]
    b_last = b[:, :, -1:]
    q_in = q * jnp.exp(b - b_mid)
    k_in = k * jnp.exp(b_mid - b)
    scores = jnp.einsum('bnihd,bnjhd->bnhij', q_in, k_in)
    causal = jnp.tril(jnp.ones((CHUNK, CHUNK), dtype=bool))
    scores = jnp.where(causal, scores, 0.0)
    o_intra = jnp.einsum('bnhij,bnjhe->bnihe', scores, v)
    k_state = k * jnp.exp(b_last - b)
    upd = jnp.einsum('bnchd,bnche->nbhde', k_state, v)
    decay = jnp.exp(b_last[:, :, 0]).transpose(1, 0, 2, 3)

    def step(state, xs):
        u, a = xs
        return a[..., None] * state + u, state

    s0 = jnp.zeros((bsz, H, DK, DV), dtype=q.dtype)
    _, s_prev = lax.scan(step, s0, (upd, decay))
    o_inter = jnp.einsum('bnchd,nbhde->bnche', q * jnp.exp(b), s_prev)
    o = (o_intra + o_inter).reshape(bsz, n_chunks * CHUNK, H, DV)
    return o[:, pad:]


def causal_dwconv(u, w):
    L = u.shape[1]
    up = jnp.pad(u, ((0, 0), (CONV_K - 1, 0), (0, 0)))
    y = w[0] * up[:, 0:L]
    for i in range(1, CONV_K):
        y = y + w[i] * up[:, i:i + L]
    return y


def hybrid_layer(h, g_pre, w_in, w_gate_up, b_gate, g_gla_out, w_conv, w_out, g_post):
    bsz, L, _ = h.shape
    xn = rmsnorm(h, g_pre)
    p = xn @ w_in
    q, k, v, z_gla, r, hc, gate_b, gate_c, z_conv = jnp.split(p, SPLIT_POINTS, axis=-1)

    log_a = jax.nn.log_sigmoid((r @ w_gate_up + b_gate).astype(jnp.float32)) / GATE_TAU
    heads = lambda t, d: t.reshape(bsz, L, GLA_HEADS, d).astype(jnp.float32)
    o = gla_chunked(heads(q, GLA_HEAD_K) * (GLA_HEAD_K ** -0.5), heads(k, GLA_HEAD_K),
                    heads(v, GLA_HEAD_V), log_a.reshape(bsz, L, GLA_HEADS, GLA_HEAD_K))
    o = rmsnorm(o, g_gla_out).reshape(bsz, L, GLA_WIDTH).astype(h.dtype)
    y_gla = o * jax.nn.silu(z_gla)

    y_conv = gate_b * causal_dwconv(gate_c * hc, w_conv) * jax.nn.silu(z_conv)

    y = jnp.concatenate([y_gla, y_conv], axis=-1) @ w_out
    return h + rmsnorm(y, g_post)


def setup_inputs(seed: int = 0) -> dict:
    key = jax.random.key(seed)
    ks = jax.random.split(key, 11)
    f32 = jnp.float32
    x = jax.random.normal(ks[0], (BATCH, SEQ, D_MODEL), f32)
    meta_tokens = jax.random.normal(ks[1], (N_META, D_MODEL), f32)
    norm_pre = 1.0 + 0.02 * jax.random.normal(ks[2], (DEPTH, D_MODEL), f32)
    w_in = jax.random.normal(ks[3], (DEPTH, D_MODEL, D_PROJ), f32) * D_MODEL ** -0.5
    w_gate_up = jax.random.normal(ks[4], (DEPTH, GATE_RANK, GLA_KEY), f32) * GATE_RANK ** -0.5
    b_gate = 0.1 * jax.random.normal(ks[5], (DEPTH, GLA_KEY), f32)
    gla_out_norm = 1.0 + 0.02 * jax.random.normal(ks[6], (DEPTH, GLA_HEAD_V), f32)
    conv_w = jax.random.normal(ks[7], (DEPTH, CONV_K, CONV_WIDTH), f32) * CONV_K ** -0.5
    w_out = jax.random.normal(ks[8], (DEPTH, D_MIX, D_MODEL), f32) * D_MIX ** -0.5
    norm_post = 1.0 + 0.02 * jax.random.normal(ks[9], (DEPTH, D_MODEL), f32)
    return {'x': x, 'meta_tokens': meta_tokens, 'norm_pre': norm_pre, 'w_in': w_in,
            'w_gate_up': w_gate_up, 'b_gate': b_gate, 'gla_out_norm': gla_out_norm,
            'conv_w': conv_w, 'w_out': w_out, 'norm_post': norm_post}


def reference(x, meta_tokens, norm_pre, w_in, w_gate_up, b_gate, gla_out_norm,
              conv_w, w_out, norm_post):
    bsz = x.shape[0]
    meta = jnp.broadcast_to(meta_tokens.astype(x.dtype)[None], (bsz, N_META, D_MODEL))
    h = jnp.concatenate([meta, x], axis=1)
    for layer in range(DEPTH):
        h = hybrid_layer(h, norm_pre[layer], w_in[layer], w_gate_up[layer], b_gate[layer],
                         gla_out_norm[layer], conv_w[layer], w_out[layer], norm_post[layer])
    return h[:, N_META:]
```

```python
import contextlib
import numpy as np
import concourse.bass as bass
import concourse.mybir as mybir
from concourse.bass_utils import run_bass_kernel_spmd

F32 = mybir.dt.float32
BF16 = mybir.dt.bfloat16
AF = mybir.ActivationFunctionType
ALU = mybir.AluOpType

D = 2048
DP = 7184
T = 512
NSUB = 4
DEPTH = 4
NMETA = 16
SEQ = 8192
EPS = 1e-6
Q0, K0, V0, Z0, R0, HC0, B0, C0, ZC0 = 0, 512, 1024, 2048, 3072, 3088, 4112, 5136, 6160
SW = 1040


class Prog:
    def __init__(self, nc, es):
        self.nc = nc
        self.es = es
        self.eng = {}
        for name, h in (("pe", nc.tensor), ("act", nc.scalar), ("dve", nc.vector),
                        ("pool", nc.gpsimd), ("sp", nc.sync)):
            self.eng[name] = dict(h=h, sem=es.enter_context(nc.semaphore("s_" + name)), n=0)
        self.waited = {}
        self.lastw = {}
        self.readers = {}
        self.dsem = {}
        self.nwait = 0

    def dma_sem(self, name):
        if name not in self.dsem:
            self.dsem[name] = [self.es.enter_context(self.nc.semaphore("d_" + name)), 0]
        return self.dsem[name]

    def _wait(self, ename, tok):
        sem, val, src = tok
        if src == ename and ename in ("pe", "sp"):
            return
        k = (ename, id(sem))
        if self.waited.get(k, 0) >= val:
            return
        self.waited[k] = val
        self.eng[ename]["h"].wait_ge(sem, val)
        self.nwait += 1

    def op(self, ename, fn, reads=(), writes=(), dma=None, same_war=False):
        e = self.eng[ename]
        deps = []
        for k in reads:
            if k in self.lastw:
                deps.append(self.lastw[k])
        for k in writes:
            if k in self.lastw:
                deps.append(self.lastw[k])
            for r in self.readers.get(k, ()):
                if r[2] == ename and dma is None and ename != "pool":
                    continue
                deps.append(r)
        for tok in deps:
            self._wait(ename, tok)
        ins = fn()
        if dma is None:
            e["n"] += 1
            ins.then_inc(e["sem"], 1)
            tok = (e["sem"], e["n"], ename)
        else:
            ds = self.dma_sem(dma)
            ds[1] += 16
            ins.then_inc(ds[0], 16)
            tok = (ds[0], ds[1], "dma:" + dma)
        for k in writes:
            self.lastw[k] = tok
            self.readers[k] = []
        for k in reads:
            self.readers.setdefault(k, []).append(tok)
        return tok

    def wait_all(self, ename):
        for k, tok in list(self.lastw.items()):
            self._wait(ename, tok)
        for k, rs in list(self.readers.items()):
            for r in rs:
                self._wait(ename, r)


def build_program(n_slots, ring, n_ranks):
    nc = bass.Bass("TRN2", target_bir_lowering=False)
    dt = nc.dram_tensor
    xin = dt("xin", [n_slots, T, D], F32, kind="ExternalInput").ap()
    w_in = dt("w_in", [DEPTH, D, DP], F32, kind="ExternalInput").ap()
    w_out = dt("w_out", [DEPTH, D, D], F32, kind="ExternalInput").ap()
    waug_d = dt("waug", [DEPTH, 17, 512], F32, kind="ExternalInput").ap()
    gpre_d = dt("gpre", [128, DEPTH * 16], F32, kind="ExternalInput").ap()
    gpost_d = dt("gpost", [DEPTH, D], F32, kind="ExternalInput")
    gout_d = dt("gout", [128, DEPTH * 2], F32, kind="ExternalInput").ap()
    cw_d = dt("cw", [128, DEPTH * 24], F32, kind="ExternalInput").ap()
    keep_d = dt("keep", [128, n_slots], F32, kind="ExternalInput").ap()
    oh_d = dt("onehot", [128, 8], F32, kind="ExternalInput").ap()
    cst_d = dt("cst", [128, 4 * 128], F32, kind="ExternalInput").ap()
    yout = dt("yout", [n_slots, T, D], F32, kind="ExternalOutput").ap()
    if ring > 1:
        st_out = dt("st_out", [128, SW], F32).ap()
        st_all = dt("st_all", [n_ranks * 128, SW], F32).ap()
    else:
        st_loc = dt("st_loc", [DEPTH, 128, SW], F32).ap()

    es = contextlib.ExitStack()
    with es:
        P = Prog(nc, es)
        sb = lambda name, shape, d: es.enter_context(nc.sbuf_tensor(name, shape, d))
        H = sb("H", [128, NSUB, D], F32)
        A = sb("A", [128, 8, 1024], F32)
        YT = sb("YT", [128, 16, 512], BF16)
        W = [sb("W0", [128, 16, 528], BF16), sb("W1", [128, 16, 528], BF16)]
        ZS = sb("ZS", [128, 8, 512], BF16)
        V = sb("V", [128, NSUB, 1024], BF16)
        QB = sb("QB", [128, 4, 512], BF16)
        QIN = sb("QIN", [128, 4, 512], BF16)
        KIN = sb("KIN", [128, 4, 512], BF16)
        KST = sb("KST", [128, NSUB, 512], BF16)
        KSTT = sb("KSTT", [128, 2, 128], BF16)
        T32 = sb("T32", [128, 2, 520], F32)
        CS = sb("CS", [128, 4, 512], F32)
        U = sb("U", [128, 4, 514], F32)
        S = sb("S", [128, 4, 256], F32)
        SBF = sb("SBF", [128, 4, 256], BF16)
        UT = sb("UT", [128, 8, 2], F32)
        DEC = sb("DEC", [128, 4, 4], F32)
        SCM = sb("SCM", [128, 2, 128], BF16)
        SQ = sb("SQ", [128, 2, 512], BF16)
        RR = sb("RR", [128, 512], F32)
        GPOST = sb("GPOST", [128, D], F32)
        WAUG = sb("WAUG", [32, 512], F32)
        RTA = sb("RTA", [32, 512], F32)
        GPRE = sb("GPRE", [128, DEPTH * 16], F32)
        GOUT = sb("GOUT", [128, DEPTH * 2], F32)
        CW = sb("CW", [128, DEPTH * 24], F32)
        KEEP = sb("KEEP", [128, n_slots], F32)
        OH = sb("OH", [128, 8], F32)
        CSTF = sb("CSTF", [128, 4 * 128], F32)
        IDENT = sb("IDENT", [128, 128], BF16)
        MASK = sb("MASK", [128, 128], BF16)
        ONES = sb("ONES", [128, 128], BF16)
        SSQ = sb("SSQ", [128, 32], F32)
        PS = [es.enter_context(nc.psum_tensor("PS%d" % i, [128, 512], F32)) for i in range(8)]
        es.enter_context(nc.Block())

        act, dve, pe, pool, sp = nc.scalar, nc.vector, nc.tensor, nc.gpsimd, nc.sync
        TRI = CSTF[:, 256:384]

        P.op("sp", lambda: sp.dma_start(out=CSTF[:], in_=cst_d[:, :]), writes=["CSTF"], dma="c0")
        P.op("sp", lambda: sp.dma_start(out=GPRE[:], in_=gpre_d[:, :]), writes=["GPRE"], dma="c1")
        P.op("sp", lambda: sp.dma_start(out=GOUT[:], in_=gout_d[:, :]), writes=["GOUT"], dma="c2")
        P.op("sp", lambda: sp.dma_start(out=CW[:], in_=cw_d[:, :]), writes=["CW"], dma="c3")
        P.op("sp", lambda: sp.dma_start(out=KEEP[:], in_=keep_d[:, :]), writes=["KEEP"], dma="c4")
        P.op("sp", lambda: sp.dma_start(out=OH[:], in_=oh_d[:, :]), writes=["OH"], dma="c5")
        P.op("dve", lambda: dve.tensor_copy(out=IDENT[:], in_=CSTF[:, 0:128]), reads=["CSTF"], writes=["IDENT"])
        P.op("dve", lambda: dve.tensor_copy(out=MASK[:], in_=CSTF[:, 128:256]), reads=["CSTF"], writes=["MASK"])
        P.op("dve", lambda: dve.tensor_copy(out=ONES[:], in_=CSTF[:, 384:512]), reads=["CSTF"], writes=["ONES"])
        P.op("dve", lambda: dve.memset(RTA[:], 1.0), writes=["RTA"])
        P.op("dve", lambda: dve.memset(H[:].rearrange("p a b -> p (a b)"), 0.0), writes=["H0", "H1", "H2", "H3"])
        P.op("dve", lambda: dve.memset(S[:].rearrange("p a b -> p (a b)"), 0.0), writes=["S0", "S1", "S2", "S3"])
        P.op("dve", lambda: dve.memset(UT[:].rearrange("p a b -> p (a b)"), 0.0), writes=["UT"])
        wstate = dict(n=0)

        def wload(dram_ap, ncols):
            i = wstate["n"] % 2
            wstate["n"] += 1
            P.op("pool", lambda: pool.dma_start(out=W[i][:, :, 0:ncols], in_=dram_ap),
                 writes=["W%d" % i], dma="w%d" % i)
            return i

        def win_ap(li, c0, ncols):
            return w_in[li, :, c0:c0 + ncols].rearrange("(kc p) c -> p kc c", p=128)

        def wout_ap(li, g):
            return w_out[li, :, g * 512:(g + 1) * 512].rearrange("(kc p) c -> p kc c", p=128)

        pstate = dict(n=0)

        def pbank():
            i = pstate["n"] % 4
            pstate["n"] += 1
            return i

        A_bf = A[:].rearrange("p a b -> p (a b)").bitcast(BF16)

        def xnT_ap(kc, lo=0, hi=512):
            return A_bf[:, kc * 512 + lo: kc * 512 + hi], "A%d" % (kc // 4)

        def sp_ap(c):
            r = 4 + c // 2
            return A[:, r, (c % 2) * 512:(c % 2) * 512 + 512], "A%d" % r

        def e_ap(hh):
            r = 6 + hh % 2
            return A[:, r, 0:512], "A%d" % r

        def einv_ap(hh):
            r = 6 + hh % 2
            return A[:, r, 512:1024], "A%d" % r

        def ysb_ap(s, lo=0, hi=D):
            return A[:, 2 * s:2 * s + 2, :].rearrange("p a b -> p (a b)")[:, lo:hi], ["A%d" % (2 * s), "A%d" % (2 * s + 1)]

        YT_flat = YT[:].rearrange("p a b -> p (a b)")

        def xntok_ap(s):
            return YT_flat[:, s * 2048:(s + 1) * 2048], "YT%d" % s

        def yT_ap(fc, lo=0, hi=512):
            return YT[:, fc, lo:hi], "YT%d" % (fc // 4)

        ALLA = ["A%d" % i for i in range(8)]

        for s in range(n_slots):
            li = s % DEPTH
            P.op("sp", lambda: sp.dma_start(out=WAUG[0:17, :], in_=waug_d[li, :, :]), writes=["WAUG"], dma="waug")
            if s > 0:
                if ring == 1:
                    if s >= DEPTH:
                        P.op("sp", lambda: sp.dma_start(out=S[:].rearrange("p a b -> p (a b)"), in_=st_loc[li, :, 0:1024]),
                             reads=["STL%d" % li], writes=["S0", "S1", "S2", "S3"], dma="sload")
                        P.op("sp", lambda: sp.dma_start(out=UT[:].rearrange("p a b -> p (a b)"), in_=st_loc[li, :, 1024:SW]),
                             reads=["STL%d" % li], writes=["UT"], dma="uload")
                    else:
                        P.op("dve", lambda: dve.memset(S[:].rearrange("p a b -> p (a b)"), 0.0), writes=["S0", "S1", "S2", "S3"])
                        P.op("dve", lambda: dve.memset(UT[:].rearrange("p a b -> p (a b)"), 0.0), writes=["UT"])
                else:
                    XT = T32[:].rearrange("p a b -> p (a b)")
                    for r in range(ring):
                        P.op("sp", lambda r=r: sp.dma_start(out=XT[:, 0:SW], in_=st_all_view(r)),
                             reads=["STALL"], writes=["T32_0", "T32_1"], dma="xt")
                        if r == 0:
                            P.op("dve", lambda: dve.tensor_scalar(out=S[:].rearrange("p a b -> p (a b)"), in0=XT[:, 0:1024],
                                                                  scalar1=OH[:, 0:1], scalar2=None, op0=ALU.mult),
                                 reads=["T32_0", "T32_1", "OH"], writes=["S0", "S1", "S2", "S3"])
                            P.op("dve", lambda: dve.tensor_scalar(out=UT[:].rearrange("p a b -> p (a b)"), in0=XT[:, 1024:SW],
                                                                  scalar1=OH[:, 0:1], scalar2=None, op0=ALU.mult),
                                 reads=["T32_0", "T32_1", "OH"], writes=["UT"])
                        else:
                            P.op("dve", lambda r=r: dve.scalar_tensor_tensor(
                                out=S[:].rearrange("p a b -> p (a b)"), in0=XT[:, 0:1024], scalar=OH[:, r:r + 1],
                                in1=S[:].rearrange("p a b -> p (a b)"), op0=ALU.mult, op1=ALU.add),
                                reads=["T32_0", "T32_1", "OH", "S0", "S1", "S2", "S3"], writes=["S0", "S1", "S2", "S3"])
                            P.op("dve", lambda r=r: dve.scalar_tensor_tensor(
                                out=UT[:].rearrange("p a b -> p (a b)"), in0=XT[:, 1024:SW], scalar=OH[:, r:r + 1],
                                in1=UT[:].rearrange("p a b -> p (a b)"), op0=ALU.mult, op1=ALU.add),
                                reads=["T32_0", "T32_1", "OH", "UT"], writes=["UT"])
            for hh in range(4):
                P.op("act", lambda hh=hh: act.copy(out=SBF[:, hh, :], in_=S[:, hh, :]), reads=["S%d" % hh], writes=["SBF%d" % hh])

            for t in range(NSUB):
                ya, yk = ysb_ap(t)
                P.op("sp", lambda t=t, ya=ya: sp.dma_start(out=ya, in_=xin[s, t * 128:(t + 1) * 128, :]), writes=yk, dma="xs%d" % t)
                P.op("dve", lambda t=t, ya=ya: dve.scalar_tensor_tensor(
                    out=H[:, t, :], in0=H[:, t, :], scalar=KEEP[:, s:s + 1], in1=ya, op0=ALU.mult, op1=ALU.add),
                    reads=yk + ["H%d" % t, "KEEP"], writes=["H%d" % t])

            for t in range(NSUB):
                xa, xk = xntok_ap(t)
                c0 = 0 + t
                P.op("act", lambda t=t, xa=xa, c0=c0: act.activation(out=xa, in_=H[:, t, :], func=AF.Square, accum_out=SSQ[:, c0:c0 + 1]),
                     reads=["H%d" % t], writes=[xk, "SSQa%d" % t])
                P.op("act", lambda c0=c0: act.activation(out=SSQ[:, 4 + c0:5 + c0], in_=SSQ[:, c0:c0 + 1], func=AF.Ln, scale=1.0 / D, bias=EPS),
                     reads=["SSQa%d" % t], writes=["SSQb%d" % t])
                P.op("act", lambda c0=c0: act.activation(out=SSQ[:, 8 + c0:9 + c0], in_=SSQ[:, 4 + c0:5 + c0], func=AF.Exp, scale=-0.5),
                     reads=["SSQb%d" % t], writes=["SSQc%d" % t])
                P.op("dve", lambda t=t, xa=xa, c0=c0: dve.tensor_scalar(out=xa, in0=H[:, t, :], scalar1=SSQ[:, 8 + c0:9 + c0], scalar2=None, op0=ALU.mult),
                     reads=["H%d" % t, "SSQc%d" % t], writes=[xk])
            for kc in range(16):
                b = pbank()
                pt = PS[b][:].bitcast(BF16)
                for t in range(NSUB):
                    xa, xk = xntok_ap(t)
                    P.op("pe", lambda t=t, xa=xa, pt=pt, kc=kc: pe.transpose(pt[:, t * 128:(t + 1) * 128], xa[:, kc * 128:(kc + 1) * 128], IDENT[:]),
                         reads=[xk, "IDENT"], writes=["P%d" % b])
                oa, ok = xnT_ap(kc)
                P.op("act", lambda oa=oa, pt=pt, kc=kc: act.activation(out=oa, in_=pt[:, 0:512], func=AF.Copy, scale=GPRE[:, li * 16 + kc:li * 16 + kc + 1]),
                     reads=["P%d" % b, "GPRE"], writes=[ok])

            XNT_KEYS = ["A0", "A1", "A2", "A3"]

            def proj_B(wi, col_lo, nblk, evac):
                for j in range(nblk):
                    b = pbank()
                    for kc in range(16):
                        ra, rk = xnT_ap(kc)
                        P.op("pe", lambda kc=kc, ra=ra, b=b, j=j: pe.matmul(PS[b][:, :], lhsT=W[wi][:, kc, col_lo + j * 128:col_lo + (j + 1) * 128],
                                                                          rhs=ra, start=(kc == 0), stop=(kc == 15)),
                             reads=[rk, "W%d" % wi], writes=["P%d" % b])
                    evac(j, PS[b], "P%d" % b)

            wi = wload(win_ap(li, Z0 + 512, 528), 528)

            def evac_z(base):
                def f(j, ps, pk):
                    P.op("act", lambda: act.activation(out=ZS[:, base + j, :], in_=ps[:, :], func=AF.Silu), reads=[pk], writes=["ZS%d" % (base + j)])
                return f
            proj_B(wi, 0, 4, evac_z(4))
            b = pbank()
            for kc in range(16):
                ra, rk = xnT_ap(kc)
                P.op("pe", lambda kc=kc, ra=ra, b=b: pe.matmul(PS[b][0:16, :], lhsT=W[wi][:, kc, 512:528], rhs=ra, start=(kc == 0), stop=(kc == 15)),
                     reads=[rk, "W%d" % wi], writes=["P%d" % b])
            P.op("act", lambda b=b: act.copy(out=RTA[0:16, :], in_=PS[b][0:16, :]), reads=["P%d" % b], writes=["RTA"])
            for c in range(NSUB):
                b = pbank()
                P.op("pe", lambda c=c, b=b: pe.matmul(PS[b][:, :], lhsT=RTA[0:17, c * 128:(c + 1) * 128], rhs=WAUG[0:17, :], start=True, stop=True),
                     reads=["RTA", "WAUG"], writes=["P%d" % b])
                sa, sk = sp_ap(c)
                P.op("act", lambda sa=sa, b=b: act.activation(out=sa, in_=PS[b][:, :], func=AF.Exp, scale=-1.0), reads=["P%d" % b], writes=[sk])
                P.op("act", lambda sa=sa: act.activation(out=sa, in_=sa, func=AF.Ln, bias=1.0), reads=[sk], writes=[sk])
            wi = wload(win_ap(li, Z0, 512), 512)
            proj_B(wi, 0, 4, evac_z(0))
            for g in range(2):
                wi = wload(win_ap(li, V0 + g * 512, 512), 512)
                for t in range(NSUB):
                    b = pbank()
                    for kc in range(16):
                        ra, rk = xnT_ap(kc, t * 128, (t + 1) * 128)
                        P.op("pe", lambda kc=kc, ra=ra, b=b, wi=wi: pe.matmul(PS[b][:, :], lhsT=ra, rhs=W[wi][:, kc, 0:512], start=(kc == 0), stop=(kc == 15)),
                             reads=[rk, "W%d" % wi], writes=["P%d" % b])
                    P.op("dve", lambda t=t, g=g, b=b: dve.tensor_copy(out=V[:, t, g * 512:(g + 1) * 512], in_=PS[b][:, :]),
                         reads=["P%d" % b], writes=["V%d" % t])
            wk = wload(win_ap(li, K0, 512), 512)
            wq = wload(win_ap(li, Q0, 512), 512)
            for hh in range(4):
                bb = pbank()
                for c in range(NSUB):
                    sa, sk = sp_ap(c)
                    P.op("pe", lambda c=c, sa=sa, bb=bb, hh=hh: pe.matmul(PS[bb][:, c * 128:(c + 1) * 128], lhsT=sa[:, hh * 128:(hh + 1) * 128], rhs=TRI, start=True, stop=True),
                         reads=[sk, "CSTF"], writes=["P%d" % bb])
                ea, ek = e_ap(hh)
                ia, ik = einv_ap(hh)
                P.op("act", lambda ea=ea, bb=bb: act.activation(out=ea, in_=PS[bb][:, :], func=AF.Exp), reads=["P%d" % bb], writes=[ek])
                P.op("act", lambda ia=ia, bb=bb: act.activation(out=ia, in_=PS[bb][:, :], func=AF.Exp, scale=-1.0), reads=["P%d" % bb], writes=[ik])
                for c in range(NSUB):
                    P.op("dve", lambda c=c, ea=ea, hh=hh: dve.tensor_copy(out=DEC[:, hh, c:c + 1], in_=ea[:, c * 128 + 127:c * 128 + 128]), reads=[ek], writes=["DEC%d" % hh])
                b = pbank()
                for kc in range(16):
                    ra, rk = xnT_ap(kc)
                    P.op("pe", lambda kc=kc, ra=ra, b=b, hh=hh: pe.matmul(PS[b][:, :], lhsT=W[wk][:, kc, hh * 128:(hh + 1) * 128], rhs=ra, start=(kc == 0), stop=(kc == 15)),
                         reads=[rk, "W%d" % wk], writes=["P%d" % b])
                P.op("dve", lambda ia=ia, b=b: dve.tensor_tensor(out=T32[:, 0, 0:512], in0=PS[b][:, :], in1=ia, op=ALU.mult), reads=["P%d" % b, ik], writes=["T32_0"])
                for c in range(NSUB):
                    cs = slice(c * 128, (c + 1) * 128)
                    P.op("dve", lambda c=c, cs=cs, ea=ea, hh=hh: dve.tensor_scalar(out=KIN[:, hh, cs], in0=T32[:, 0, cs], scalar1=ea[:, c * 128 + 63:c * 128 + 64], scalar2=None, op0=ALU.mult),
                         reads=["T32_0", ek], writes=["KIN%d" % hh])
                    j = (hh * 4 + c) % 2
                    P.op("dve", lambda c=c, cs=cs, ea=ea, j=j: dve.tensor_scalar(out=KSTT[:, j, :], in0=T32[:, 0, cs], scalar1=ea[:, c * 128 + 127:c * 128 + 128], scalar2=None, op0=ALU.mult),
                         reads=["T32_0", ek], writes=["KSTT%d" % j])
                    tb = 6
                    tk = "P6"
                    pt6 = PS[6][:].bitcast(BF16)
                    P.op("pe", lambda c=c, j=j, pt6=pt6: pe.transpose(pt6[:, (c % 4) * 128:(c % 4) * 128 + 128], KSTT[:, j, :], IDENT[:]),
                         reads=["KSTT%d" % j, "IDENT"], writes=[tk])
                    P.op("act", lambda c=c, hh=hh, pt6=pt6: act.copy(out=KST[:, c, hh * 128:(hh + 1) * 128], in_=pt6[:, (c % 4) * 128:(c % 4) * 128 + 128]),
                         reads=[tk], writes=["KST%d" % c])
                b = pbank()
                for kc in range(16):
                    ra, rk = xnT_ap(kc)
                    P.op("pe", lambda kc=kc, ra=ra, b=b, hh=hh: pe.matmul(PS[b][:, :], lhsT=W[wq][:, kc, hh * 128:(hh + 1) * 128], rhs=ra, start=(kc == 0), stop=(kc == 15)),
                         reads=[rk, "W%d" % wq], writes=["P%d" % b])
                P.op("dve", lambda ea=ea, b=b: dve.scalar_tensor_tensor(out=T32[:, 1, 0:512], in0=PS[b][:, :], scalar=float(128 ** -0.5), in1=ea, op0=ALU.mult, op1=ALU.mult),
                     reads=["P%d" % b, ek], writes=["T32_1"])
                P.op("act", lambda hh=hh: act.copy(out=QB[:, hh, :], in_=T32[:, 1, 0:512]), reads=["T32_1"], writes=["QB%d" % hh])
                for c in range(NSUB):
                    cs = slice(c * 128, (c + 1) * 128)
                    P.op("dve", lambda c=c, cs=cs, ia=ia, hh=hh: dve.tensor_scalar(out=QIN[:, hh, cs], in0=T32[:, 1, cs], scalar1=ia[:, c * 128 + 63:c * 128 + 64], scalar2=None, op0=ALU.mult),
                         reads=["T32_1", ik], writes=["QIN%d" % hh])

            for hh in range(4):
                for c in range(NSUB):
                    cs = slice(c * 128, (c + 1) * 128)
                    j = (hh * 4 + c) % 2
                    sk6 = "P7"
                    sc_ps = PS[7][:, j * 128:(j + 1) * 128]
                    P.op("pe", lambda cs=cs, hh=hh, sc_ps=sc_ps: pe.matmul(sc_ps, lhsT=KIN[:, hh, cs], rhs=QIN[:, hh, cs], start=True, stop=True),
                         reads=["KIN%d" % hh, "QIN%d" % hh], writes=[sk6])
                    P.op("dve", lambda j=j, sc_ps=sc_ps: dve.tensor_tensor(out=SCM[:, j, :], in0=sc_ps, in1=MASK[:], op=ALU.mult),
                         reads=[sk6, "MASK"], writes=["SCM%d" % j])
                    for eh in range(2):
                        ob = 4 + eh
                        P.op("pe", lambda c=c, hh=hh, eh=eh, ob=ob, cs=cs, j=j: pe.matmul(PS[ob][:, cs], lhsT=V[:, c, hh * 256 + eh * 128:hh * 256 + eh * 128 + 128], rhs=SCM[:, j, :], start=True, stop=False),
                             reads=["V%d" % c, "SCM%d" % j], writes=["P%d" % ob])
                        P.op("pe", lambda hh=hh, eh=eh, ob=ob, cs=cs: pe.matmul(PS[ob][:, cs], lhsT=SBF[:, hh, eh * 128:(eh + 1) * 128], rhs=QB[:, hh, cs], start=False, stop=True),
                             reads=["SBF%d" % hh, "QB%d" % hh], writes=["P%d" % ob])
                    uk = "P7u%d" % j
                    up_ps = PS[7][:, 256 + j * 128: 256 + j * 128 + 128]
                    for eh in range(2):
                        ukk = "P7"
                        ups = PS[7][:, 256 + eh * 128:256 + (eh + 1) * 128]
                        P.op("pe", lambda c=c, hh=hh, eh=eh, ups=ups: pe.matmul(ups, lhsT=KST[:, c, hh * 128:(hh + 1) * 128], rhs=V[:, c, hh * 256 + eh * 128:hh * 256 + (eh + 1) * 128], start=True, stop=True),
                             reads=["KST%d" % c, "V%d" % c], writes=[ukk])
                    P.op("dve", lambda c=c, hh=hh: dve.scalar_tensor_tensor(out=S[:, hh, :], in0=S[:, hh, :], scalar=DEC[:, hh, c:c + 1], in1=PS[7][:, 256:512], op0=ALU.mult, op1=ALU.add),
                         reads=["S%d" % hh, "DEC%d" % hh, "P7"], writes=["S%d" % hh])
                    P.op("act", lambda hh=hh: act.copy(out=SBF[:, hh, :], in_=S[:, hh, :]), reads=["S%d" % hh], writes=["SBF%d" % hh])
                for eh in range(2):
                    P.op("act", lambda eh=eh: act.activation(out=SQ[:, eh, :], in_=PS[4 + eh][:, :], func=AF.Square), reads=["P%d" % (4 + eh)], writes=["SQ%d" % eh])
                b = pbank()
                for eh in range(2):
                    P.op("pe", lambda eh=eh, b=b: pe.matmul(PS[b][:, :], lhsT=ONES[:], rhs=SQ[:, eh, :], start=(eh == 0), stop=(eh == 1)),
                         reads=["ONES", "SQ%d" % eh], writes=["P%d" % b])
                P.op("act", lambda b=b: act.activation(out=RR[:], in_=PS[b][:, :], func=AF.Ln, scale=1.0 / 256, bias=EPS), reads=["P%d" % b], writes=["RR"])
                P.op("act", lambda: act.activation(out=RR[:], in_=RR[:], func=AF.Exp, scale=-0.5), reads=["RR"], writes=["RR"])
                for eh in range(2):
                    fc = hh * 2 + eh
                    P.op("dve", lambda eh=eh, hh=hh: dve.scalar_tensor_tensor(out=T32[:, eh, 0:512], in0=PS[4 + eh][:, :], scalar=GOUT[:, li * 2 + eh:li * 2 + eh + 1], in1=RR[:], op0=ALU.mult, op1=ALU.mult),
                         reads=["P%d" % (4 + eh), "GOUT", "RR"], writes=["T32_%d" % eh])
                    ya, yk = yT_ap(fc)
                    P.op("dve", lambda eh=eh, fc=fc, ya=ya: dve.tensor_tensor(out=ya, in0=T32[:, eh, 0:512], in1=ZS[:, fc, :], op=ALU.mult),
                         reads=["T32_%d" % eh, "ZS%d" % fc], writes=[yk])

            for half in range(2):
                cb0 = half * 4
                P.op("dve", lambda cb0=cb0: dve.tensor_copy(out=U[:, :, 0:2], in_=UT[:, cb0:cb0 + 4, :]), reads=["UT"], writes=["U0", "U1", "U2", "U3"])
                wi = wload(win_ap(li, C0 + half * 512, 512), 512)

                def ev_c(j, ps, pk):
                    P.op("act", lambda: act.copy(out=CS[:, j, :], in_=ps[:, :]), reads=[pk], writes=["CS%d" % j])
                proj_B(wi, 0, 4, ev_c)
                wi = wload(win_ap(li, HC0 + half * 512, 512), 512)

                def ev_hc(j, ps, pk, cb0=cb0):
                    cbi = cb0 + j
                    P.op("dve", lambda: dve.tensor_tensor(out=U[:, j, 2:514], in0=ps[:, :], in1=CS[:, j, :], op=ALU.mult), reads=[pk, "CS%d" % j], writes=["U%d" % j])
                    P.op("dve", lambda: dve.tensor_copy(out=UT[:, cbi, :], in_=U[:, j, 512:514]), reads=["U%d" % j], writes=["UT"])
                    w0 = CW[:, li * 24 + cbi * 3 + 0:li * 24 + cbi * 3 + 1]
                    w1 = CW[:, li * 24 + cbi * 3 + 1:li * 24 + cbi * 3 + 2]
                    w2 = CW[:, li * 24 + cbi * 3 + 2:li * 24 + cbi * 3 + 3]
                    P.op("dve", lambda: dve.tensor_scalar(out=CS[:, j, :], in0=U[:, j, 0:512], scalar1=w0, scalar2=None, op0=ALU.mult), reads=["U%d" % j, "CW"], writes=["CS%d" % j])
                    P.op("dve", lambda: dve.scalar_tensor_tensor(out=CS[:, j, :], in0=U[:, j, 1:513], scalar=w1, in1=CS[:, j, :], op0=ALU.mult, op1=ALU.add), reads=["U%d" % j, "CW", "CS%d" % j], writes=["CS%d" % j])
                    P.op("dve", lambda: dve.scalar_tensor_tensor(out=CS[:, j, :], in0=U[:, j, 2:514], scalar=w2, in1=CS[:, j, :], op0=ALU.mult, op1=ALU.add), reads=["U%d" % j, "CW", "CS%d" % j], writes=["CS%d" % j])
                proj_B(wi, 0, 4, ev_hc)
                wi = wload(win_ap(li, B0 + half * 512, 512), 512)

                def ev_b(j, ps, pk):
                    P.op("dve", lambda: dve.tensor_tensor(out=CS[:, j, :], in0=ps[:, :], in1=CS[:, j, :], op=ALU.mult), reads=[pk, "CS%d" % j], writes=["CS%d" % j])
                proj_B(wi, 0, 4, ev_b)
                wi = wload(win_ap(li, ZC0 + half * 512, 512), 512)

                def ev_zc(j, ps, pk, cb0=cb0):
                    e = j % 2
                    P.op("act", lambda: act.activation(out=T32[:, e, 0:512], in_=ps[:, :], func=AF.Silu), reads=[pk], writes=["T32_%d" % e])
                    ya, yk = yT_ap(8 + cb0 + j)
                    P.op("dve", lambda: dve.tensor_tensor(out=ya, in0=CS[:, j, :], in1=T32[:, e, 0:512], op=ALU.mult), reads=["CS%d" % j, "T32_%d" % e], writes=[yk])
                proj_B(wi, 0, 4, ev_zc)

            S_flat = S[:].rearrange("p a b -> p (a b)")
            UT_flat = UT[:].rearrange("p a b -> p (a b)")
            if ring == 1:
                P.op("sp", lambda: sp.dma_start(out=st_loc[li, :, 0:1024], in_=S_flat), reads=["S0", "S1", "S2", "S3"], writes=["STL%d" % li], dma="stl%d" % li)
                P.op("sp", lambda: sp.dma_start(out=st_loc[li, :, 1024:SW], in_=UT_flat), reads=["UT"], writes=["STL%d" % li], dma="stl%d" % li)
            elif s < n_slots - 1:
                P.op("sp", lambda: sp.dma_start(out=st_out[:, 0:1024], in_=S_flat), reads=["S0", "S1", "S2", "S3"], writes=["STOUT"], dma="sto")
                P.op("sp", lambda: sp.dma_start(out=st_out[:, 1024:SW], in_=UT_flat), reads=["UT"], writes=["STOUT"], dma="sto")
                groups = [list(range(g * ring, (g + 1) * ring)) for g in range(n_ranks // ring)]
                P.op("pool", lambda: pool.collective_compute("AllGather", ALU.bypass, groups, [st_out[:, :]], [st_all[0:ring * 128, :]]),
                     reads=["STOUT"], writes=["STALL"], dma="cc")

            P.op("sp", lambda: sp.dma_start(out=GPOST[:], in_=bass.AP(gpost_d, li * D, [[0, 128], [1, D]])), writes=["GPOST"], dma="gpost")
            for g in range(4):
                wi = wload(wout_ap(li, g), 512)
                for t in range(NSUB):
                    b = pbank()
                    for fc in range(16):
                        ya, yk = yT_ap(fc, t * 128, (t + 1) * 128)
                        P.op("pe", lambda fc=fc, ya=ya, b=b, wi=wi: pe.matmul(PS[b][:, :], lhsT=ya, rhs=W[wi][:, fc, 0:512], start=(fc == 0), stop=(fc == 15)),
                             reads=[yk, "W%d" % wi], writes=["P%d" % b])
                    ya2, yk2 = ysb_ap(t, g * 512, (g + 1) * 512)
                    P.op("act", lambda b=b, ya2=ya2: act.copy(out=ya2, in_=PS[b][:, :]), reads=["P%d" % b], writes=yk2)
                    P.op("act", lambda b=b, t=t, g=g: act.activation(out=SQ[:, 0, :], in_=PS[b][:, :], func=AF.Square, accum_out=SSQ[:, 12 + t * 4 + g:13 + t * 4 + g]),
                         reads=["P%d" % b], writes=["SQ0", "SSQd%d" % t])
            for t in range(NSUB):
                P.op("dve", lambda t=t: dve.tensor_reduce(out=SSQ[:, 28 + t:29 + t], in_=SSQ[:, 12 + t * 4:16 + t * 4], axis=mybir.AxisListType.X, op=ALU.add),
                     reads=["SSQd%d" % t], writes=["SSQe%d" % t])
                P.op("act", lambda t=t: act.activation(out=SSQ[:, 28 + t:29 + t], in_=SSQ[:, 28 + t:29 + t], func=AF.Ln, scale=1.0 / D, bias=EPS), reads=["SSQe%d" % t], writes=["SSQe%d" % t])
                P.op("act", lambda t=t: act.activation(out=SSQ[:, 28 + t:29 + t], in_=SSQ[:, 28 + t:29 + t], func=AF.Exp, scale=-0.5), reads=["SSQe%d" % t], writes=["SSQe%d" % t])
                ya, yk = ysb_ap(t)
                P.op("dve", lambda t=t, ya=ya: dve.scalar_tensor_tensor(out=ya, in0=ya, scalar=SSQ[:, 28 + t:29 + t], in1=GPOST[:], op0=ALU.mult, op1=ALU.mult),
                     reads=yk + ["SSQe%d" % t, "GPOST"], writes=yk)
                P.op("dve", lambda t=t, ya=ya: dve.tensor_tensor(out=H[:, t, :], in0=H[:, t, :], in1=ya, op=ALU.add), reads=yk + ["H%d" % t], writes=["H%d" % t])
                P.op("sp", lambda t=t: sp.dma_start(out=yout[s, t * 128:(t + 1) * 128, :], in_=H[:, t, :]), reads=["H%d" % t], writes=["YOUT"], dma="yo%d" % t)

        P.wait_all("sp")
    return nc


def st_all_view(r):
    raise NotImplementedError


def _consts():
    ident = np.eye(128, dtype=np.float32)
    jj, ii = np.meshgrid(np.arange(128), np.arange(128), indexing="ij")
    mask = (jj <= ii).astype(np.float32)
    tri = mask * np.float32(-1.0 / 16.0)
    ones = np.ones((128, 128), np.float32)
    return np.ascontiguousarray(np.concatenate([ident, mask, tri, ones], axis=1))


def _layout_params(norm_pre, w_gate_up, b_gate, gla_out_norm, conv_w, norm_post, rot):
    order = [(i - rot) % DEPTH for i in range(DEPTH)]
    gpre = np.concatenate([norm_pre[l].reshape(16, 128).T for l in order], axis=1)
    gout = np.concatenate([gla_out_norm[l].reshape(2, 128).T for l in order], axis=1)
    cw = np.concatenate([conv_w[l].reshape(3, 8, 128).transpose(2, 1, 0).reshape(128, 24) for l in order], axis=1)
    waug = np.stack([np.concatenate([w_gate_up[l], b_gate[l][None, :]], axis=0) for l in order], axis=0)
    gpost = np.stack([norm_post[l] for l in order], axis=0)
    f = lambda a: np.ascontiguousarray(a, dtype=np.float32)
    return f(gpre), f(gout), f(cw), f(waug), f(gpost), order


def run_ring1(x, meta_tokens, norm_pre, w_in, w_gate_up, b_gate, gla_out_norm, conv_w, w_out, norm_post, n_tok_tiles):
    x = np.asarray(x, np.float32)
    B = x.shape[0]
    n_tiles = 1 + n_tok_tiles
    ring = 1
    n_cores = B
    n_slots = n_tiles * DEPTH
    nc = build_program(n_slots, ring, n_cores)
    cst = _consts()
    gpre, gout, cw, waug, gpost, order = _layout_params(norm_pre, w_gate_up, b_gate, gla_out_norm, conv_w, norm_post, 0)
    w_in_r = np.ascontiguousarray(np.asarray(w_in, np.float32))
    w_out_r = np.ascontiguousarray(np.asarray(w_out, np.float32))
    in_maps = []
    for c in range(n_cores):
        xin = np.zeros((n_slots, T, D), np.float32)
        keep = np.ones((128, n_slots), np.float32)
        for j in range(n_tiles):
            s0 = j * DEPTH
            keep[:, s0] = 0.0
            if j == 0:
                xin[s0, T - NMETA:, :] = meta_tokens
            else:
                xin[s0] = x[c, (j - 1) * T:j * T, :]
        oh = np.zeros((128, 8), np.float32)
        in_maps.append(dict(xin=xin, w_in=w_in_r, w_out=w_out_r, waug=waug, gpre=gpre, gpost=gpost,
                            gout=gout, cw=cw, keep=keep, onehot=oh, cst=cst))
    res = run_bass_kernel_spmd(nc, in_maps, core_ids=list(range(n_cores)))
    out = np.empty((B, n_tok_tiles * T, D), np.float32)
    for c in range(n_cores):
        yo = res.results[c]["yout"]
        for j in range(1, n_tiles):
            out[c, (j - 1) * T:j * T, :] = yo[j * DEPTH + DEPTH - 1]
    return out, res


def kernel(x, meta_tokens, norm_pre, w_in, w_gate_up, b_gate, gla_out_norm, conv_w, w_out, norm_post):
    out, _ = run_ring1(x, meta_tokens, norm_pre, w_in, w_gate_up, b_gate, gla_out_norm, conv_w, w_out, norm_post, SEQ // T)
    return out
```

```python
import contextlib
import numpy as np
import concourse.bass as bass
import concourse.mybir as mybir
from concourse.bass_utils import run_bass_kernel_spmd

F32 = mybir.dt.float32
BF16 = mybir.dt.bfloat16
AF = mybir.ActivationFunctionType
ALU = mybir.AluOpType

D = 2048
DP = 7184
T = 512
NSUB = 4
DEPTH = 4
NMETA = 16
SEQ = 8192
EPS = 1e-6
Q0, K0, V0, Z0, R0, HC0, B0, C0, ZC0 = 0, 512, 1024, 2048, 3072, 3088, 4112, 5136, 6160
SW = 1040


class Prog:
    def __init__(self, nc, es):
        self.nc = nc
        self.es = es
        self.eng = {}
        for name, h in (("pe", nc.tensor), ("act", nc.scalar), ("dve", nc.vector),
                        ("pool", nc.gpsimd), ("sp", nc.sync)):
            self.eng[name] = dict(h=h, sem=es.enter_context(nc.semaphore("s_" + name)), n=0)
        self.waited = {}
        self.lastw = {}
        self.readers = {}
        self.dsem = {}
        self.nwait = 0

    def dma_sem(self, name):
        if name not in self.dsem:
            self.dsem[name] = [self.es.enter_context(self.nc.semaphore("d_" + name)), 0]
        return self.dsem[name]

    def _wait(self, ename, tok):
        sem, val, src = tok
        if src == ename and ename in ("pe", "sp"):
            return
        k = (ename, id(sem))
        if self.waited.get(k, 0) >= val:
            return
        self.waited[k] = val
        self.eng[ename]["h"].wait_ge(sem, val)
        self.nwait += 1

    def op(self, ename, fn, reads=(), writes=(), dma=None, same_war=False):
        e = self.eng[ename]
        deps = []
        for k in reads:
            if k in self.lastw:
                deps.append(self.lastw[k])
        for k in writes:
            if k in self.lastw:
                deps.append(self.lastw[k])
            for r in self.readers.get(k, ()):
                if r[2] == ename and dma is None and ename != "pool":
                    continue
                deps.append(r)
        for tok in deps:
            self._wait(ename, tok)
        ins = fn()
        if dma is None:
            e["n"] += 1
            ins.then_inc(e["sem"], 1)
            tok = (e["sem"], e["n"], ename)
        else:
            ds = self.dma_sem(dma)
            ds[1] += 16
            ins.then_inc(ds[0], 16)
            tok = (ds[0], ds[1], "dma:" + dma)
        for k in writes:
            self.lastw[k] = tok
            self.readers[k] = []
        for k in reads:
            self.readers.setdefault(k, []).append(tok)
        return tok

    def wait_all(self, ename):
        for k, tok in list(self.lastw.items()):
            self._wait(ename, tok)
        for k, rs in list(self.readers.items()):
            for r in rs:
                self._wait(ename, r)


def build_program(n_slots, ring, n_ranks, n_xt=None):
    nc = bass.Bass("TRN2", target_bir_lowering=False)
    dt = nc.dram_tensor
    if n_xt is None:
        n_xt = n_slots
    xin = dt("xin", [n_xt, T, D], F32, kind="ExternalInput").ap()
    w_in = [dt("w_in%d" % l, [D, DP], F32, kind="ExternalInput").ap() for l in range(DEPTH)]
    w_out = [dt("w_out%d" % l, [D, D], F32, kind="ExternalInput").ap() for l in range(DEPTH)]
    waug_d = dt("waug", [DEPTH, 17, 512], F32, kind="ExternalInput").ap()
    gpre_d = dt("gpre", [128, DEPTH * 16], F32, kind="ExternalInput").ap()
    gpost_d = dt("gpost", [DEPTH, D], F32, kind="ExternalInput")
    gout_d = dt("gout", [128, DEPTH * 2], F32, kind="ExternalInput").ap()
    cw_d = dt("cw", [128, DEPTH * 24], F32, kind="ExternalInput").ap()
    keep_d = dt("keep", [128, 2 * n_slots], F32, kind="ExternalInput").ap()
    oh_d = dt("onehot", [128, 8], F32, kind="ExternalInput").ap()
    cst_d = dt("cst", [128, 4 * 128], F32, kind="ExternalInput").ap()
    yout = dt("yout", [n_slots // DEPTH - 1, T, D], F32, kind="ExternalOutput").ap()
    if ring > 1:
        st_out = dt("st_out", [128, SW], F32).ap()
        st_all = dt("st_all", [ring * 128, SW], F32).ap()

        def st_all_view(r):
            return st_all[r * 128:(r + 1) * 128, :]
    else:
        st_loc = dt("st_loc", [DEPTH, 128, SW], F32).ap()

    es = contextlib.ExitStack()
    with es:
        P = Prog(nc, es)
        sb = lambda name, shape, d: es.enter_context(nc.sbuf_tensor(name, shape, d))
        H = sb("H", [128, NSUB, D], F32)
        A = sb("A", [128, 8, 1024], F32)
        YT = sb("YT", [128, 16, 512], BF16)
        W = [sb("W0", [128, 16, 528], BF16), sb("W1", [128, 16, 528], BF16)]
        ZS = sb("ZS", [128, 8, 512], BF16)
        V = sb("V", [128, NSUB, 1024], BF16)
        QB = sb("QB", [128, 4, 512], BF16)
        QIN = sb("QIN", [128, 4, 512], BF16)
        KIN = sb("KIN", [128, 4, 512], BF16)
        KST = sb("KST", [128, NSUB, 512], BF16)
        KSTT = sb("KSTT", [128, 2, 4, 128], BF16)
        YTMP = sb("YTMP", [128, 2, 512], F32)
        T32 = sb("T32", [128, 2, 520], F32)
        CS = sb("CS", [128, 4, 512], F32)
        U = sb("U", [128, 4, 514], F32)
        S = sb("S", [128, 4, 256], F32)
        SBF = sb("SBF", [128, 4, 256], BF16)
        UT = sb("UT", [128, 8, 2], F32)
        DEC = sb("DEC", [128, 4, 4], F32)
        SCM = sb("SCM", [128, 2, 128], BF16)
        SQ = sb("SQ", [128, 2, 512], BF16)
        RR = sb("RR", [128, 512], F32)
        GPOST = sb("GPOST", [128, D], F32)
        WAUG = sb("WAUG", [32, 512], F32)
        RTA = sb("RTA", [32, 512], F32)
        GPRE = sb("GPRE", [128, DEPTH * 16], F32)
        GOUT = sb("GOUT", [128, DEPTH * 2], F32)
        CW = sb("CW", [128, DEPTH * 24], F32)
        KEEP = sb("KEEP", [128, 2 * n_slots], F32)
        OH = sb("OH", [128, 8], F32)
        CSTF = sb("CSTF", [128, 4 * 128], F32)
        IDENT = sb("IDENT", [128, 128], BF16)
        MASK = sb("MASK", [128, 128], BF16)
        ONES = sb("ONES", [128, 128], BF16)
        SSQ = sb("SSQ", [128, 32], F32)
        PS = [es.enter_context(nc.psum_tensor("PS%d" % i, [128, 512], F32)) for i in range(8)]
        es.enter_context(nc.Block())

        act, dve, pe, pool, sp = nc.scalar, nc.vector, nc.tensor, nc.gpsimd, nc.sync
        TRI = CSTF[:, 256:384]

        P.op("sp", lambda: sp.dma_start(out=CSTF[:], in_=cst_d[:, :]), writes=["CSTF"], dma="c0")
        P.op("sp", lambda: sp.dma_start(out=GPRE[:], in_=gpre_d[:, :]), writes=["GPRE"], dma="c1")
        P.op("sp", lambda: sp.dma_start(out=GOUT[:], in_=gout_d[:, :]), writes=["GOUT"], dma="c2")
        P.op("sp", lambda: sp.dma_start(out=CW[:], in_=cw_d[:, :]), writes=["CW"], dma="c3")
        P.op("sp", lambda: sp.dma_start(out=KEEP[:], in_=keep_d[:, :]), writes=["KEEP"], dma="c4")
        P.op("sp", lambda: sp.dma_start(out=OH[:], in_=oh_d[:, :]), writes=["OH"], dma="c5")
        P.op("dve", lambda: dve.tensor_copy(out=IDENT[:], in_=CSTF[:, 0:128]), reads=["CSTF"], writes=["IDENT"])
        P.op("dve", lambda: dve.tensor_copy(out=MASK[:], in_=CSTF[:, 128:256]), reads=["CSTF"], writes=["MASK"])
        P.op("dve", lambda: dve.tensor_copy(out=ONES[:], in_=CSTF[:, 384:512]), reads=["CSTF"], writes=["ONES"])
        P.op("dve", lambda: dve.memset(RTA[:], 1.0), writes=["RTA"])
        P.op("dve", lambda: dve.memset(H[:].rearrange("p a b -> p (a b)"), 0.0), writes=["H0", "H1", "H2", "H3"])
        P.op("dve", lambda: dve.memset(S[:].rearrange("p a b -> p (a b)"), 0.0), writes=["S0", "S1", "S2", "S3"])
        P.op("dve", lambda: dve.memset(UT[:].rearrange("p a b -> p (a b)"), 0.0), writes=["UT"])
        wstate = dict(n=0)

        def wload(dram_ap, ncols):
            i = wstate["n"] % 2
            wstate["n"] += 1
            P.op("pool", lambda: pool.dma_start(out=W[i][:, :, 0:ncols], in_=dram_ap),
                 writes=["W%d" % i], dma="w%d" % i)
            return i

        def win_ap(li, c0, ncols):
            return w_in[li][:, c0:c0 + ncols].rearrange("(kc p) c -> p kc c", p=128)

        def wout_ap(li, g):
            return w_out[li][:, g * 512:(g + 1) * 512].rearrange("(kc p) c -> p kc c", p=128)

        pstate = dict(n=0)

        def pbank():
            i = pstate["n"] % 4
            pstate["n"] += 1
            return i

        A_bf = A[:].rearrange("p a b -> p (a b)").bitcast(BF16)

        def xnT_ap(kc, lo=0, hi=512):
            return A_bf[:, kc * 512 + lo: kc * 512 + hi], "A%d" % (kc // 4)

        def sp_ap(c):
            r = 4 + c // 2
            return A[:, r, (c % 2) * 512:(c % 2) * 512 + 512], "A%d" % r

        def e_ap(hh):
            r = 6 + hh % 2
            return A[:, r, 0:512], "A%d" % r

        def einv_ap(hh):
            r = 6 + hh % 2
            return A[:, r, 512:1024], "A%d" % r

        def ysb_ap(s, lo=0, hi=D):
            return A[:, 2 * s:2 * s + 2, :].rearrange("p a b -> p (a b)")[:, lo:hi], ["A%d" % (2 * s), "A%d" % (2 * s + 1)]

        YT_flat = YT[:].rearrange("p a b -> p (a b)")

        def xntok_ap(s):
            return YT_flat[:, s * 2048:(s + 1) * 2048], "YT%d" % s

        def yT_ap(fc, lo=0, hi=512):
            return YT[:, fc, lo:hi], "YT%d" % (fc // 4)

        ALLA = ["A%d" % i for i in range(8)]

        for s in range(n_slots):
            li = s % DEPTH
            P.op("sp", lambda: sp.dma_start(out=WAUG[0:17, :], in_=waug_d[li, :, :]), writes=["WAUG"], dma="waug")
            if s > 0:
                if ring == 1:
                    if s >= DEPTH:
                        P.op("sp", lambda: sp.dma_start(out=S[:].rearrange("p a b -> p (a b)"), in_=st_loc[li, :, 0:1024]),
                             reads=["STL%d" % li], writes=["S0", "S1", "S2", "S3"], dma="sload")
                        P.op("sp", lambda: sp.dma_start(out=UT[:].rearrange("p a b -> p (a b)"), in_=st_loc[li, :, 1024:SW]),
                             reads=["STL%d" % li], writes=["UT"], dma="uload")
                    else:
                        P.op("dve", lambda: dve.memset(S[:].rearrange("p a b -> p (a b)"), 0.0), writes=["S0", "S1", "S2", "S3"])
                        P.op("dve", lambda: dve.memset(UT[:].rearrange("p a b -> p (a b)"), 0.0), writes=["UT"])
                else:
                    XT = T32[:].rearrange("p a b -> p (a b)")
                    for r in range(ring):
                        P.op("sp", lambda r=r: sp.dma_start(out=XT[:, 0:SW], in_=st_all_view(r)),
                             reads=["STALL"], writes=["T32_0", "T32_1"], dma="xt")
                        if r == 0:
                            P.op("dve", lambda: dve.tensor_scalar(out=S[:].rearrange("p a b -> p (a b)"), in0=XT[:, 0:1024],
                                                                  scalar1=OH[:, 0:1], scalar2=None, op0=ALU.mult),
                                 reads=["T32_0", "T32_1", "OH"], writes=["S0", "S1", "S2", "S3"])
                            P.op("dve", lambda: dve.tensor_scalar(out=UT[:].rearrange("p a b -> p (a b)"), in0=XT[:, 1024:SW],
                                                                  scalar1=OH[:, 0:1], scalar2=None, op0=ALU.mult),
                                 reads=["T32_0", "T32_1", "OH"], writes=["UT"])
                        else:
                            P.op("dve", lambda r=r: dve.scalar_tensor_tensor(
                                out=S[:].rearrange("p a b -> p (a b)"), in0=XT[:, 0:1024], scalar=OH[:, r:r + 1],
                                in1=S[:].rearrange("p a b -> p (a b)"), op0=ALU.mult, op1=ALU.add),
                                reads=["T32_0", "T32_1", "OH", "S0", "S1", "S2", "S3"], writes=["S0", "S1", "S2", "S3"])
                            P.op("dve", lambda r=r: dve.scalar_tensor_tensor(
                                out=UT[:].rearrange("p a b -> p (a b)"), in0=XT[:, 1024:SW], scalar=OH[:, r:r + 1],
                                in1=UT[:].rearrange("p a b -> p (a b)"), op0=ALU.mult, op1=ALU.add),
                                reads=["T32_0", "T32_1", "OH", "UT"], writes=["UT"])
            for hh in range(4):
                P.op("act", lambda hh=hh: act.copy(out=SBF[:, hh, :], in_=S[:, hh, :]), reads=["S%d" % hh], writes=["SBF%d" % hh])

            assert ring == 1
            if li == 0:
                for t in range(NSUB):
                    P.op("sp", lambda t=t: sp.dma_start(out=H[:, t, :], in_=xin[s // DEPTH, t * 128:(t + 1) * 128, :]),
                         writes=["H%d" % t], dma="xs%d" % t)

            for t in range(NSUB):
                xa, xk = xntok_ap(t)
                c0 = 0 + t
                P.op("act", lambda t=t, xa=xa, c0=c0: act.activation(out=xa, in_=H[:, t, :], func=AF.Square, accum_out=SSQ[:, c0:c0 + 1]),
                     reads=["H%d" % t], writes=[xk, "SSQa%d" % t])
                P.op("act", lambda c0=c0: act.activation(out=SSQ[:, 4 + c0:5 + c0], in_=SSQ[:, c0:c0 + 1], func=AF.Ln, scale=1.0 / D, bias=EPS),
                     reads=["SSQa%d" % t], writes=["SSQb%d" % t])
                P.op("act", lambda c0=c0: act.activation(out=SSQ[:, 8 + c0:9 + c0], in_=SSQ[:, 4 + c0:5 + c0], func=AF.Exp, scale=-0.5),
                     reads=["SSQb%d" % t], writes=["SSQc%d" % t])
                P.op("dve", lambda t=t, xa=xa, c0=c0: dve.tensor_scalar(out=xa, in0=H[:, t, :], scalar1=SSQ[:, 8 + c0:9 + c0], scalar2=None, op0=ALU.mult),
                     reads=["H%d" % t, "SSQc%d" % t], writes=[xk])
            for kc in range(16):
                b = pbank()
                pt = PS[b][:].bitcast(BF16)
                for t in range(NSUB):
                    xa, xk = xntok_ap(t)
                    P.op("pe", lambda t=t, xa=xa, pt=pt, kc=kc: pe.transpose(pt[:, t * 128:(t + 1) * 128], xa[:, kc * 128:(kc + 1) * 128], IDENT[:]),
                         reads=[xk, "IDENT"], writes=["P%d" % b])
                oa, ok = xnT_ap(kc)
                P.op("act", lambda oa=oa, pt=pt, kc=kc: act.activation(out=oa, in_=pt[:, 0:512], func=AF.Copy, scale=GPRE[:, li * 16 + kc:li * 16 + kc + 1]),
                     reads=["P%d" % b, "GPRE"], writes=[ok])

            XNT_KEYS = ["A0", "A1", "A2", "A3"]

            def proj_B(wi, col_lo, nblk, evac):
                for j in range(nblk):
                    b = pbank()
                    for kc in range(16):
                        ra, rk = xnT_ap(kc)
                        P.op("pe", lambda kc=kc, ra=ra, b=b, j=j: pe.matmul(PS[b][:, :], lhsT=W[wi][:, kc, col_lo + j * 128:col_lo + (j + 1) * 128],
                                                                          rhs=ra, start=(kc == 0), stop=(kc == 15)),
                             reads=[rk, "W%d" % wi], writes=["P%d" % b])
                    evac(j, PS[b], "P%d" % b)

            wi = wload(win_ap(li, Z0 + 512, 528), 528)

            def evac_z(base):
                def f(j, ps, pk):
                    P.op("act", lambda: act.activation(out=ZS[:, base + j, :], in_=ps[:, :], func=AF.Silu), reads=[pk], writes=["ZS%d" % (base + j)])
                return f
            proj_B(wi, 0, 4, evac_z(4))
            b = pbank()
            for kc in range(16):
                ra, rk = xnT_ap(kc)
                P.op("pe", lambda kc=kc, ra=ra, b=b: pe.matmul(PS[b][0:16, :], lhsT=W[wi][:, kc, 512:528], rhs=ra, start=(kc == 0), stop=(kc == 15)),
                     reads=[rk, "W%d" % wi], writes=["P%d" % b])
            P.op("act", lambda b=b: act.copy(out=RTA[0:16, :], in_=PS[b][0:16, :]), reads=["P%d" % b], writes=["RTA"])
            for c in range(NSUB):
                b = pbank()
                P.op("pe", lambda c=c, b=b: pe.matmul(PS[b][:, :], lhsT=RTA[0:17, c * 128:(c + 1) * 128], rhs=WAUG[0:17, :], start=True, stop=True),
                     reads=["RTA", "WAUG"], writes=["P%d" % b])
                sa, sk = sp_ap(c)
                P.op("act", lambda sa=sa, b=b: act.activation(out=sa, in_=PS[b][:, :], func=AF.Exp, scale=-1.0), reads=["P%d" % b], writes=[sk])
                P.op("act", lambda sa=sa: act.activation(out=sa, in_=sa, func=AF.Ln, bias=1.0), reads=[sk], writes=[sk])
            wi = wload(win_ap(li, Z0, 512), 512)
            proj_B(wi, 0, 4, evac_z(0))
            for g in range(2):
                wi = wload(win_ap(li, V0 + g * 512, 512), 512)
                for t in range(NSUB):
                    b = pbank()
                    for kc in range(16):
                        ra, rk = xnT_ap(kc, t * 128, (t + 1) * 128)
                        P.op("pe", lambda kc=kc, ra=ra, b=b, wi=wi: pe.matmul(PS[b][:, :], lhsT=ra, rhs=W[wi][:, kc, 0:512], start=(kc == 0), stop=(kc == 15)),
                             reads=[rk, "W%d" % wi], writes=["P%d" % b])
                    P.op("dve", lambda t=t, g=g, b=b: dve.tensor_copy(out=V[:, t, g * 512:(g + 1) * 512], in_=PS[b][:, :]),
                         reads=["P%d" % b], writes=["V%d" % t])
            wk = wload(win_ap(li, K0, 512), 512)
            wq = wload(win_ap(li, Q0, 512), 512)
            pt6 = PS[6][:].bitcast(BF16)

            def emit_bT(hh):
                bb = pbank()
                for c in range(NSUB):
                    sa, sk = sp_ap(c)
                    P.op("pe", lambda c=c, sa=sa, bb=bb, hh=hh: pe.matmul(PS[bb][:, c * 128:(c + 1) * 128], lhsT=sa[:, hh * 128:(hh + 1) * 128], rhs=TRI, start=True, stop=True),
                         reads=[sk, "CSTF"], writes=["P%d" % bb])
                ea, ek = e_ap(hh)
                ia, ik = einv_ap(hh)
                P.op("act", lambda ea=ea, bb=bb: act.activation(out=ea, in_=PS[bb][:, :], func=AF.Exp), reads=["P%d" % bb], writes=[ek])
                P.op("act", lambda ia=ia, bb=bb: act.activation(out=ia, in_=PS[bb][:, :], func=AF.Exp, scale=-1.0), reads=["P%d" % bb], writes=[ik])
                P.op("dve", lambda ea=ea, hh=hh: dve.tensor_copy(out=DEC[:, hh, :], in_=ea.rearrange("p (c t) -> p c t", t=128)[:, :, 127]), reads=[ek], writes=["DEC%d" % hh])

            def emit_kq(hh):
                ea, ek = e_ap(hh)
                ia, ik = einv_ap(hh)
                b = pbank()
                for kc in range(16):
                    ra, rk = xnT_ap(kc)
                    P.op("pe", lambda kc=kc, ra=ra, b=b, hh=hh: pe.matmul(PS[b][:, :], lhsT=W[wk][:, kc, hh * 128:(hh + 1) * 128], rhs=ra, start=(kc == 0), stop=(kc == 15)),
                         reads=[rk, "W%d" % wk], writes=["P%d" % b])
                P.op("dve", lambda ia=ia, b=b: dve.tensor_tensor(out=T32[:, 0, 0:512], in0=PS[b][:, :], in1=ia, op=ALU.mult), reads=["P%d" % b, ik], writes=["T32_0"])
                for c in range(NSUB):
                    cs = slice(c * 128, (c + 1) * 128)
                    P.op("dve", lambda c=c, cs=cs, ea=ea, hh=hh: dve.tensor_scalar(out=KIN[:, hh, cs], in0=T32[:, 0, cs], scalar1=ea[:, c * 128 + 63:c * 128 + 64], scalar2=None, op0=ALU.mult),
                         reads=["T32_0", ek], writes=["KIN%d" % hh])
                    P.op("dve", lambda c=c, cs=cs, ea=ea, hh=hh: dve.tensor_scalar(out=KSTT[:, hh % 2, c, :], in0=T32[:, 0, cs], scalar1=ea[:, c * 128 + 127:c * 128 + 128], scalar2=None, op0=ALU.mult),
                         reads=["T32_0", ek], writes=["KSTT%d" % (hh % 2)])
                b = pbank()
                for kc in range(16):
                    ra, rk = xnT_ap(kc)
                    P.op("pe", lambda kc=kc, ra=ra, b=b, hh=hh: pe.matmul(PS[b][:, :], lhsT=W[wq][:, kc, hh * 128:(hh + 1) * 128], rhs=ra, start=(kc == 0), stop=(kc == 15)),
                         reads=[rk, "W%d" % wq], writes=["P%d" % b])
                P.op("dve", lambda ea=ea, b=b: dve.scalar_tensor_tensor(out=T32[:, 1, 0:512], in0=PS[b][:, :], scalar=float(128 ** -0.5), in1=ea, op0=ALU.mult, op1=ALU.mult),
                     reads=["P%d" % b, ek], writes=["T32_1"])
                P.op("act", lambda hh=hh: act.copy(out=QB[:, hh, :], in_=T32[:, 1, 0:512]), reads=["T32_1"], writes=["QB%d" % hh])
                for c in range(NSUB):
                    cs = slice(c * 128, (c + 1) * 128)
                    P.op("dve", lambda c=c, cs=cs, ia=ia, hh=hh: dve.tensor_scalar(out=QIN[:, hh, cs], in0=T32[:, 1, cs], scalar1=ia[:, c * 128 + 63:c * 128 + 64], scalar2=None, op0=ALU.mult),
                         reads=["T32_1", ik], writes=["QIN%d" % hh])

            def emit_tr(hh):
                for c in range(NSUB):
                    P.op("pe", lambda c=c, hh=hh: pe.transpose(pt6[:, c * 128:(c + 1) * 128], KSTT[:, hh % 2, c, :], IDENT[:]),
                         reads=["KSTT%d" % (hh % 2), "IDENT"], writes=["P6"])
                P.op("act", lambda hh=hh: act.copy(out=KST[:, :, hh * 128:(hh + 1) * 128], in_=pt6[:, 0:512].rearrange("p (c t) -> p c t", t=128)),
                     reads=["P6"], writes=["KST0", "KST1", "KST2", "KST3"])

            emit_bT(0); emit_bT(1); emit_kq(0); emit_kq(1); emit_tr(0); emit_bT(2); emit_kq(2); emit_tr(1)
            emit_bT(3); emit_kq(3); emit_tr(2); emit_tr(3)

            def gla_gen():
                for hh in range(4):
                    for c in range(NSUB):
                        cs = slice(c * 128, (c + 1) * 128)
                        j = (hh * 4 + c) % 2
                        sc_ps = PS[7][:, j * 128:(j + 1) * 128]
                        P.op("pe", lambda cs=cs, hh=hh, sc_ps=sc_ps: pe.matmul(sc_ps, lhsT=KIN[:, hh, cs], rhs=QIN[:, hh, cs], start=True, stop=True),
                             reads=["KIN%d" % hh, "QIN%d" % hh], writes=["P7"])
                        P.op("dve", lambda j=j, sc_ps=sc_ps: dve.tensor_tensor(out=SCM[:, j, :], in0=sc_ps, in1=MASK[:], op=ALU.mult),
                             reads=["P7", "MASK"], writes=["SCM%d" % j])
                        yield
                        for eh in range(2):
                            ob = 4 + eh
                            P.op("pe", lambda c=c, hh=hh, eh=eh, ob=ob, cs=cs, j=j: pe.matmul(PS[ob][:, cs], lhsT=V[:, c, hh * 256 + eh * 128:hh * 256 + eh * 128 + 128], rhs=SCM[:, j, :], start=True, stop=False),
                                 reads=["V%d" % c, "SCM%d" % j], writes=["P%d" % ob])
                            P.op("pe", lambda hh=hh, eh=eh, ob=ob, cs=cs: pe.matmul(PS[ob][:, cs], lhsT=SBF[:, hh, eh * 128:(eh + 1) * 128], rhs=QB[:, hh, cs], start=False, stop=True),
                                 reads=["SBF%d" % hh, "QB%d" % hh], writes=["P%d" % ob])
                        for eh in range(2):
                            ups = PS[7][:, 256 + eh * 128:256 + (eh + 1) * 128]
                            P.op("pe", lambda c=c, hh=hh, eh=eh, ups=ups: pe.matmul(ups, lhsT=KST[:, c, hh * 128:(hh + 1) * 128], rhs=V[:, c, hh * 256 + eh * 128:hh * 256 + (eh + 1) * 128], start=True, stop=True),
                                 reads=["KST%d" % c, "V%d" % c], writes=["P7"])
                        P.op("dve", lambda c=c, hh=hh: dve.scalar_tensor_tensor(out=S[:, hh, :], in0=S[:, hh, :], scalar=DEC[:, hh, c:c + 1], in1=PS[7][:, 256:512], op0=ALU.mult, op1=ALU.add),
                             reads=["S%d" % hh, "DEC%d" % hh, "P7"], writes=["S%d" % hh])
                        P.op("act", lambda hh=hh: act.copy(out=SBF[:, hh, :], in_=S[:, hh, :]), reads=["S%d" % hh], writes=["SBF%d" % hh])
                        yield
                    for eh in range(2):
                        P.op("act", lambda eh=eh: act.activation(out=SQ[:, eh, :], in_=PS[4 + eh][:, :], func=AF.Square), reads=["P%d" % (4 + eh)], writes=["SQ%d" % eh])
                    yield
                    b = pbank()
                    for eh in range(2):
                        P.op("pe", lambda eh=eh, b=b: pe.matmul(PS[b][:, :], lhsT=ONES[:], rhs=SQ[:, eh, :], start=(eh == 0), stop=(eh == 1)),
                             reads=["ONES", "SQ%d" % eh], writes=["P%d" % b])
                    P.op("act", lambda b=b: act.activation(out=RR[:], in_=PS[b][:, :], func=AF.Ln, scale=1.0 / 256, bias=EPS), reads=["P%d" % b], writes=["RR"])
                    P.op("act", lambda: act.activation(out=RR[:], in_=RR[:], func=AF.Exp, scale=-0.5), reads=["RR"], writes=["RR"])
                    for eh in range(2):
                        fc = hh * 2 + eh
                        P.op("dve", lambda eh=eh, hh=hh: dve.scalar_tensor_tensor(out=YTMP[:, eh, :], in0=PS[4 + eh][:, :], scalar=GOUT[:, li * 2 + eh:li * 2 + eh + 1], in1=RR[:], op0=ALU.mult, op1=ALU.mult),
                             reads=["P%d" % (4 + eh), "GOUT", "RR"], writes=["YTMP%d" % eh])
                        ya, yk = yT_ap(fc)
                        P.op("dve", lambda eh=eh, fc=fc, ya=ya: dve.tensor_tensor(out=ya, in0=YTMP[:, eh, :], in1=ZS[:, fc, :], op=ALU.mult),
                             reads=["YTMP%d" % eh, "ZS%d" % fc], writes=[yk])
                    yield

            def conv_gen():
                for half in range(2):
                    cb0 = half * 4
                    P.op("dve", lambda cb0=cb0: dve.tensor_copy(out=U[:, :, 0:2], in_=UT[:, cb0:cb0 + 4, :]), reads=["UT"], writes=["U0", "U1", "U2", "U3"])

                    def ev_c(j, ps, pk):
                        P.op("act", lambda: act.copy(out=CS[:, j, :], in_=ps[:, :]), reads=[pk], writes=["CS%d" % j])

                    def ev_hc(j, ps, pk, cb0=cb0):
                        cbi = cb0 + j
                        P.op("dve", lambda: dve.tensor_tensor(out=U[:, j, 2:514], in0=ps[:, :], in1=CS[:, j, :], op=ALU.mult), reads=[pk, "CS%d" % j], writes=["U%d" % j])
                        P.op("dve", lambda: dve.tensor_copy(out=UT[:, cbi, :], in_=U[:, j, 512:514]), reads=["U%d" % j], writes=["UT"])
                        w0 = CW[:, li * 24 + cbi * 3 + 0:li * 24 + cbi * 3 + 1]
                        w1 = CW[:, li * 24 + cbi * 3 + 1:li * 24 + cbi * 3 + 2]
                        w2 = CW[:, li * 24 + cbi * 3 + 2:li * 24 + cbi * 3 + 3]
                        P.op("dve", lambda: dve.tensor_scalar(out=CS[:, j, :], in0=U[:, j, 0:512], scalar1=w0, scalar2=None, op0=ALU.mult), reads=["U%d" % j, "CW"], writes=["CS%d" % j])
                        P.op("dve", lambda: dve.scalar_tensor_tensor(out=CS[:, j, :], in0=U[:, j, 1:513], scalar=w1, in1=CS[:, j, :], op0=ALU.mult, op1=ALU.add), reads=["U%d" % j, "CW", "CS%d" % j], writes=["CS%d" % j])
                        P.op("dve", lambda: dve.scalar_tensor_tensor(out=CS[:, j, :], in0=U[:, j, 2:514], scalar=w2, in1=CS[:, j, :], op0=ALU.mult, op1=ALU.add), reads=["U%d" % j, "CW", "CS%d" % j], writes=["CS%d" % j])

                    def ev_b(j, ps, pk):
                        P.op("dve", lambda: dve.tensor_tensor(out=CS[:, j, :], in0=ps[:, :], in1=CS[:, j, :], op=ALU.mult), reads=[pk, "CS%d" % j], writes=["CS%d" % j])

                    def ev_zc(j, ps, pk, cb0=cb0):
                        e = j % 2
                        P.op("act", lambda: act.activation(out=T32[:, e, 0:512], in_=ps[:, :], func=AF.Silu), reads=[pk], writes=["T32_%d" % e])
                        ya, yk = yT_ap(8 + cb0 + j)
                        P.op("dve", lambda: dve.tensor_tensor(out=ya, in0=CS[:, j, :], in1=T32[:, e, 0:512], op=ALU.mult), reads=["CS%d" % j, "T32_%d" % e], writes=[yk])

                    for col0, ev in ((C0, ev_c), (HC0, ev_hc), (B0, ev_b), (ZC0, ev_zc)):
                        wi = wload(win_ap(li, col0 + half * 512, 512), 512)
                        for j in range(4):
                            b = pbank()
                            for kc in range(16):
                                ra, rk = xnT_ap(kc)
                                P.op("pe", lambda kc=kc, ra=ra, b=b, j=j, wi=wi: pe.matmul(PS[b][:, :], lhsT=W[wi][:, kc, j * 128:(j + 1) * 128], rhs=ra, start=(kc == 0), stop=(kc == 15)),
                                     reads=[rk, "W%d" % wi], writes=["P%d" % b])
                            ev(j, PS[b], "P%d" % b)
                            yield

            g1, g2 = gla_gen(), conv_gen()
            live1 = live2 = True
            while live1 or live2:
                if live1:
                    live1 = next(g1, "END") != "END"
                if live2:
                    live2 = next(g2, "END") != "END"

            S_flat = S[:].rearrange("p a b -> p (a b)")
            UT_flat = UT[:].rearrange("p a b -> p (a b)")
            if ring == 1:
                P.op("sp", lambda: sp.dma_start(out=st_loc[li, :, 0:1024], in_=S_flat), reads=["S0", "S1", "S2", "S3"], writes=["STL%d" % li], dma="stl%d" % li)
                P.op("sp", lambda: sp.dma_start(out=st_loc[li, :, 1024:SW], in_=UT_flat), reads=["UT"], writes=["STL%d" % li], dma="stl%d" % li)
            elif s < n_slots - 1:
                P.op("sp", lambda: sp.dma_start(out=st_out[:, 0:1024], in_=S_flat), reads=["S0", "S1", "S2", "S3"], writes=["STOUT"], dma="sto")
                P.op("sp", lambda: sp.dma_start(out=st_out[:, 1024:SW], in_=UT_flat), reads=["UT"], writes=["STOUT"], dma="sto")
                groups = [list(range(g * ring, (g + 1) * ring)) for g in range(n_ranks // ring)]
                P.op("pool", lambda: pool.collective_compute("AllGather", ALU.bypass, groups, [st_out[:, :]], [st_all[0:ring * 128, :]]),
                     reads=["STOUT"], writes=["STALL"], dma="cc")

            P.op("sp", lambda: sp.dma_start(out=GPOST[:], in_=bass.AP(gpost_d, li * D, [[0, 128], [1, D]])), writes=["GPOST"], dma="gpost")
            for g in range(4):
                wi = wload(wout_ap(li, g), 512)
                for t in range(NSUB):
                    b = pbank()
                    for fc in range(16):
                        ya, yk = yT_ap(fc, t * 128, (t + 1) * 128)
                        P.op("pe", lambda fc=fc, ya=ya, b=b, wi=wi: pe.matmul(PS[b][:, :], lhsT=ya, rhs=W[wi][:, fc, 0:512], start=(fc == 0), stop=(fc == 15)),
                             reads=[yk, "W%d" % wi], writes=["P%d" % b])
                    ya2, yk2 = ysb_ap(t, g * 512, (g + 1) * 512)
                    P.op("act", lambda b=b, ya2=ya2: act.copy(out=ya2, in_=PS[b][:, :]), reads=["P%d" % b], writes=yk2)
                    P.op("act", lambda b=b, t=t, g=g: act.activation(out=SQ[:, 0, :], in_=PS[b][:, :], func=AF.Square, accum_out=SSQ[:, 12 + t * 4 + g:13 + t * 4 + g]),
                         reads=["P%d" % b], writes=["SQ0", "SSQd%d" % t])
            for t in range(NSUB):
                P.op("dve", lambda t=t: dve.tensor_reduce(out=SSQ[:, 28 + t:29 + t], in_=SSQ[:, 12 + t * 4:16 + t * 4], axis=mybir.AxisListType.X, op=ALU.add),
                     reads=["SSQd%d" % t], writes=["SSQe%d" % t])
                P.op("act", lambda t=t: act.activation(out=SSQ[:, 28 + t:29 + t], in_=SSQ[:, 28 + t:29 + t], func=AF.Ln, scale=1.0 / D, bias=EPS), reads=["SSQe%d" % t], writes=["SSQe%d" % t])
                P.op("act", lambda t=t: act.activation(out=SSQ[:, 28 + t:29 + t], in_=SSQ[:, 28 + t:29 + t], func=AF.Exp, scale=-0.5), reads=["SSQe%d" % t], writes=["SSQe%d" % t])
                ya, yk = ysb_ap(t)
                P.op("dve", lambda t=t, ya=ya: dve.scalar_tensor_tensor(out=ya, in0=ya, scalar=SSQ[:, 28 + t:29 + t], in1=GPOST[:], op0=ALU.mult, op1=ALU.mult),
                     reads=yk + ["SSQe%d" % t, "GPOST"], writes=yk)
                P.op("dve", lambda t=t, ya=ya: dve.tensor_tensor(out=H[:, t, :], in0=H[:, t, :], in1=ya, op=ALU.add), reads=yk + ["H%d" % t], writes=["H%d" % t])
                if li == DEPTH - 1 and s // DEPTH >= 1:
                    P.op("sp", lambda t=t: sp.dma_start(out=yout[s // DEPTH - 1, t * 128:(t + 1) * 128, :], in_=H[:, t, :]), reads=["H%d" % t], writes=["YOUT"], dma="yo%d" % t)

        P.wait_all("sp")
    return nc


def _consts():
    ident = np.eye(128, dtype=np.float32)
    jj, ii = np.meshgrid(np.arange(128), np.arange(128), indexing="ij")
    mask = (jj <= ii).astype(np.float32)
    tri = mask * np.float32(-1.0 / 16.0)
    ones = np.ones((128, 128), np.float32)
    return np.ascontiguousarray(np.concatenate([ident, mask, tri, ones], axis=1))


def _layout_params(norm_pre, w_gate_up, b_gate, gla_out_norm, conv_w, norm_post, rot):
    order = [(i - rot) % DEPTH for i in range(DEPTH)]
    gpre = np.concatenate([norm_pre[l].reshape(16, 128).T for l in order], axis=1)
    gout = np.concatenate([gla_out_norm[l].reshape(2, 128).T for l in order], axis=1)
    cw = np.concatenate([conv_w[l].reshape(3, 8, 128).transpose(2, 1, 0).reshape(128, 24) for l in order], axis=1)
    waug = np.stack([np.concatenate([w_gate_up[l], b_gate[l][None, :]], axis=0) for l in order], axis=0)
    gpost = np.stack([norm_post[l] for l in order], axis=0)
    f = lambda a: np.ascontiguousarray(a, dtype=np.float32)
    return f(gpre), f(gout), f(cw), f(waug), f(gpost), order


def run_ring1(x, meta_tokens, norm_pre, w_in, w_gate_up, b_gate, gla_out_norm, conv_w, w_out, norm_post, n_tok_tiles):
    x = np.asarray(x, np.float32)
    meta_tokens = np.asarray(meta_tokens, np.float32)
    B = x.shape[0]
    n_tiles = 1 + n_tok_tiles
    n_cores = B
    n_slots = n_tiles * DEPTH
    nc = build_program(n_slots, 1, n_cores, n_tiles)
    cst = _consts()
    gpre, gout, cw, waug, gpost, order = _layout_params(norm_pre, w_gate_up, b_gate, gla_out_norm, conv_w, norm_post, 0)
    w_in = np.asarray(w_in, np.float32)
    w_out = np.asarray(w_out, np.float32)
    in_maps = []
    for c in range(n_cores):
        xin = np.zeros((n_tiles, T, D), np.float32)
        xin[0, T - NMETA:, :] = meta_tokens
        xin[1:] = x[c].reshape(n_tok_tiles, T, D)
        keep = np.ones((128, 2 * n_slots), np.float32)
        oh = np.zeros((128, 8), np.float32)
        mp = dict(xin=xin, waug=waug, gpre=gpre, gpost=gpost, gout=gout, cw=cw, keep=keep, onehot=oh, cst=cst)
        for l in range(DEPTH):
            mp["w_in%d" % l] = np.ascontiguousarray(w_in[l])
            mp["w_out%d" % l] = np.ascontiguousarray(w_out[l])
        in_maps.append(mp)
    res = run_bass_kernel_spmd(nc, in_maps, core_ids=list(range(n_cores)))
    out = np.empty((B, n_tok_tiles * T, D), np.float32)
    for c in range(n_cores):
        out[c] = res.results[c]["yout"].reshape(n_tok_tiles * T, D)
    return out, res


def run_ring(x, meta_tokens, norm_pre, w_in, w_gate_up, b_gate, gla_out_norm, conv_w, w_out, norm_post, n_tok_tiles, ring=4):
    x = np.asarray(x, np.float32)
    meta_tokens = np.asarray(meta_tokens, np.float32)
    B = x.shape[0]
    n_tiles = 1 + n_tok_tiles
    n_cores = B * ring
    n_slots = n_tiles + ring - 1
    n_xt = (n_tiles + ring - 1) // ring
    nc = build_program(n_slots, ring, n_cores, n_xt)
    cst = _consts()
    w_in = np.asarray(w_in, np.float32)
    w_out = np.asarray(w_out, np.float32)
    per_rp = []
    for rp in range(ring):
        gpre, gout, cw, waug, gpost, order = _layout_params(norm_pre, w_gate_up, b_gate, gla_out_norm, conv_w, norm_post, rp)
        keep = np.zeros((128, 2 * n_slots), np.float32)
        for s in range(n_slots):
            m = s - rp
            if m >= 0 and m % DEPTH != 0:
                keep[:, s] = 1.0
            if m >= 0 and m % DEPTH == 0 and (DEPTH * 0 + ring * (m // DEPTH) + rp) < n_tiles:
                keep[:, n_slots + s] = 1.0
        oh = np.zeros((128, 8), np.float32)
        oh[:, (rp - 1) % ring] = 1.0
        mp = dict(waug=waug, gpre=gpre, gpost=gpost, gout=gout, cw=cw, keep=keep, onehot=oh, cst=cst)
        for l in range(DEPTH):
            mp["w_in%d" % l] = np.ascontiguousarray(w_in[order[l]])
            mp["w_out%d" % l] = np.ascontiguousarray(w_out[order[l]])
        per_rp.append(mp)
    in_maps = []
    for c in range(n_cores):
        b, rp = c // ring, c % ring
        xin = np.zeros((n_xt, T, D), np.float32)
        for i in range(n_xt):
            j = ring * i + rp
            if j == 0:
                xin[i, T - NMETA:, :] = meta_tokens
            elif j < n_tiles:
                xin[i] = x[b, (j - 1) * T:j * T, :]
        mp = dict(per_rp[rp])
        mp["xin"] = xin
        in_maps.append(mp)
    res = run_bass_kernel_spmd(nc, in_maps, core_ids=list(range(n_cores)))
    out = np.empty((B, n_tok_tiles * T, D), np.float32)
    for c in range(n_cores):
        b, rp = c // ring, c % ring
        yo = res.results[c]["yout"]
        for i in range(n_xt):
            j = ring * i + rp
            if 1 <= j < n_tiles:
                out[b, (j - 1) * T:j * T, :] = yo[DEPTH * i + rp]
    return out, res


def kernel(x, meta_tokens, norm_pre, w_in, w_gate_up, b_gate, gla_out_norm, conv_w, w_out, norm_post):
    out, _ = run_ring1(x, meta_tokens, norm_pre, w_in, w_gate_up, b_gate, gla_out_norm, conv_w, w_out, norm_post, SEQ // T)
    return out
```

```python
import contextlib
import numpy as np
import concourse.bass as bass
import concourse.mybir as mybir
from concourse.bass_utils import run_bass_kernel_spmd

F32 = mybir.dt.float32
BF16 = mybir.dt.bfloat16
AF = mybir.ActivationFunctionType
ALU = mybir.AluOpType

D = 2048
DP = 7184
T = 512
NSUB = 4
DEPTH = 4
NMETA = 16
SEQ = 8192
EPS = 1e-6
Q0, K0, V0, Z0, R0, HC0, B0, C0, ZC0 = 0, 512, 1024, 2048, 3072, 3088, 4112, 5136, 6160
SW = 1040


class Prog:
    def __init__(self, nc, es):
        self.nc = nc
        self.es = es
        self.eng = {}
        for name, h in (("pe", nc.tensor), ("act", nc.scalar), ("dve", nc.vector),
                        ("pool", nc.gpsimd), ("sp", nc.sync)):
            self.eng[name] = dict(h=h, sem=es.enter_context(nc.semaphore("s_" + name)), n=0)
        self.waited = {}
        self.lastw = {}
        self.readers = {}
        self.dsem = {}
        self.nwait = 0

    def dma_sem(self, name):
        if name not in self.dsem:
            self.dsem[name] = [self.es.enter_context(self.nc.semaphore("d_" + name)), 0]
        return self.dsem[name]

    def _wait(self, ename, tok):
        sem, val, src = tok
        if src == ename and ename in ("pe", "sp"):
            return
        k = (ename, id(sem))
        if self.waited.get(k, 0) >= val:
            return
        self.waited[k] = val
        self.eng[ename]["h"].wait_ge(sem, val)
        self.nwait += 1

    def op(self, ename, fn, reads=(), writes=(), dma=None, same_war=False):
        e = self.eng[ename]
        deps = []
        for k in reads:
            if k in self.lastw:
                deps.append(self.lastw[k])
        for k in writes:
            if k in self.lastw:
                deps.append(self.lastw[k])
            for r in self.readers.get(k, ()):
                if r[2] == ename and dma is None and ename != "pool":
                    continue
                deps.append(r)
        for tok in deps:
            self._wait(ename, tok)
        ins = fn()
        if dma is None:
            e["n"] += 1
            ins.then_inc(e["sem"], 1)
            tok = (e["sem"], e["n"], ename)
        else:
            ds = self.dma_sem(dma)
            ds[1] += 16
            ins.then_inc(ds[0], 16)
            tok = (ds[0], ds[1], "dma:" + dma)
        for k in writes:
            self.lastw[k] = tok
            self.readers[k] = []
        for k in reads:
            self.readers.setdefault(k, []).append(tok)
        return tok

    def wait_all(self, ename):
        for k, tok in list(self.lastw.items()):
            self._wait(ename, tok)
        for k, rs in list(self.readers.items()):
            for r in rs:
                self._wait(ename, r)


def build_program(n_slots, ring, n_ranks, n_xt=None):
    nc = bass.Bass("TRN2", target_bir_lowering=False)
    dt = nc.dram_tensor
    if n_xt is None:
        n_xt = n_slots
    xin = dt("xin", [n_xt, T, D], F32, kind="ExternalInput").ap()
    w_in = [dt("w_in%d" % l, [D, DP], F32, kind="ExternalInput").ap() for l in range(DEPTH)]
    w_out = [dt("w_out%d" % l, [D, D], F32, kind="ExternalInput").ap() for l in range(DEPTH)]
    waug_d = dt("waug", [DEPTH, 17, 512], F32, kind="ExternalInput").ap()
    gpre_d = dt("gpre", [128, DEPTH * 16], F32, kind="ExternalInput").ap()
    gpost_d = dt("gpost", [DEPTH, D], F32, kind="ExternalInput")
    gout_d = dt("gout", [128, DEPTH * 2], F32, kind="ExternalInput").ap()
    cw_d = dt("cw", [128, DEPTH * 24], F32, kind="ExternalInput").ap()
    keep_d = dt("keep", [128, 2 * n_slots], F32, kind="ExternalInput").ap()
    oh_d = dt("onehot", [128, 8], F32, kind="ExternalInput").ap()
    cst_d = dt("cst", [128, 4 * 128], F32, kind="ExternalInput").ap()
    yout = dt("yout", [n_slots // DEPTH - 1, T, D], F32, kind="ExternalOutput").ap()
    if ring > 1:
        st_out = dt("st_out", [128, SW], F32).ap()
        st_all = dt("st_all", [ring * 128, SW], F32).ap()

        def st_all_view(r):
            return st_all[r * 128:(r + 1) * 128, :]
    else:
        st_loc = dt("st_loc", [DEPTH, 128, SW], F32).ap()

    es = contextlib.ExitStack()
    with es:
        P = Prog(nc, es)
        sb = lambda name, shape, d: es.enter_context(nc.sbuf_tensor(name, shape, d))
        H = sb("H", [128, NSUB, D], F32)
        A = sb("A", [128, 8, 1024], F32)
        YT = sb("YT", [128, 16, 512], BF16)
        W = [sb("W0", [128, 16, 528], BF16), sb("W1", [128, 16, 528], BF16), sb("W2", [128, 16, 528], BF16)]
        ZS = sb("ZS", [128, 8, 512], BF16)
        V = sb("V", [128, NSUB, 1024], BF16)
        QB = sb("QB", [128, 4, 512], BF16)
        QIN = sb("QIN", [128, 4, 512], BF16)
        KIN = sb("KIN", [128, 4, 512], BF16)
        KST = sb("KST", [128, NSUB, 512], BF16)
        KSTT = sb("KSTT", [128, 2, 4, 128], BF16)
        YTMP = sb("YTMP", [128, 2, 512], F32)
        T32 = sb("T32", [128, 2, 520], F32)
        CS = sb("CS", [128, 4, 512], F32)
        U = sb("U", [128, 4, 514], F32)
        S = sb("S", [128, 4, 256], F32)
        SBF = sb("SBF", [128, 4, 256], BF16)
        UT = sb("UT", [128, 8, 2], F32)
        DEC = sb("DEC", [128, 4, 4], F32)
        SCM = sb("SCM", [128, 2, 128], BF16)
        SQ = sb("SQ", [128, 2, 512], BF16)
        RR = sb("RR", [128, 512], F32)
        WAUG = sb("WAUG", [32, 512], F32)
        RTA = sb("RTA", [32, 512], F32)
        GPRE = sb("GPRE", [128, DEPTH * 16], F32)
        GOUT = sb("GOUT", [128, DEPTH * 2], F32)
        CW = sb("CW", [128, DEPTH * 24], F32)
        KEEP = sb("KEEP", [128, 2 * n_slots], F32)
        OH = sb("OH", [128, 8], F32)
        CSTF = sb("CSTF", [128, 4 * 128], F32)
        GPOST = CS[:].rearrange("p a b -> p (a b)")
        CSK = ["CS0", "CS1", "CS2", "CS3"]
        IDENT = sb("IDENT", [128, 128], BF16)
        MASK = sb("MASK", [128, 128], BF16)
        ONES = sb("ONES", [128, 128], BF16)
        SSQ = sb("SSQ", [128, 32], F32)
        PS = [es.enter_context(nc.psum_tensor("PS%d" % i, [128, 512], F32)) for i in range(8)]
        es.enter_context(nc.Block())

        act, dve, pe, pool, sp = nc.scalar, nc.vector, nc.tensor, nc.gpsimd, nc.sync
        TRI = CSTF[:, 256:384]

        P.op("sp", lambda: sp.dma_start(out=CSTF[:], in_=cst_d[:, :]), writes=["CSTF"], dma="c0")
        P.op("sp", lambda: sp.dma_start(out=GPRE[:], in_=gpre_d[:, :]), writes=["GPRE"], dma="c1")
        P.op("sp", lambda: sp.dma_start(out=GOUT[:], in_=gout_d[:, :]), writes=["GOUT"], dma="c2")
        P.op("sp", lambda: sp.dma_start(out=CW[:], in_=cw_d[:, :]), writes=["CW"], dma="c3")
        P.op("sp", lambda: sp.dma_start(out=KEEP[:], in_=keep_d[:, :]), writes=["KEEP"], dma="c4")
        P.op("sp", lambda: sp.dma_start(out=OH[:], in_=oh_d[:, :]), writes=["OH"], dma="c5")
        P.op("dve", lambda: dve.tensor_copy(out=IDENT[:], in_=CSTF[:, 0:128]), reads=["CSTF"], writes=["IDENT"])
        P.op("dve", lambda: dve.tensor_copy(out=MASK[:], in_=CSTF[:, 128:256]), reads=["CSTF"], writes=["MASK"])
        P.op("dve", lambda: dve.tensor_copy(out=ONES[:], in_=CSTF[:, 384:512]), reads=["CSTF"], writes=["ONES"])
        P.op("dve", lambda: dve.memset(RTA[:], 1.0), writes=["RTA"])
        P.op("dve", lambda: dve.memset(H[:].rearrange("p a b -> p (a b)"), 0.0), writes=["H0", "H1", "H2", "H3"])
        P.op("dve", lambda: dve.memset(S[:].rearrange("p a b -> p (a b)"), 0.0), writes=["S0", "S1", "S2", "S3"])
        P.op("dve", lambda: dve.memset(UT[:].rearrange("p a b -> p (a b)"), 0.0), writes=["UT"])
        wstate = dict(n=0)

        def wload(dram_ap, ncols):
            i = wstate["n"] % len(W)
            wstate["n"] += 1
            P.op("pool", lambda: pool.dma_start(out=W[i][:, :, 0:ncols], in_=dram_ap),
                 writes=["W%d" % i], dma="w%d" % i)
            return i

        def win_ap(li, c0, ncols):
            return w_in[li][:, c0:c0 + ncols].rearrange("(kc p) c -> p kc c", p=128)

        def wout_ap(li, g):
            return w_out[li][:, g * 512:(g + 1) * 512].rearrange("(kc p) c -> p kc c", p=128)

        pstate = dict(n=0)

        def pbank():
            i = pstate["n"] % 4
            pstate["n"] += 1
            return i

        A_bf = A[:].rearrange("p a b -> p (a b)").bitcast(BF16)

        def xnT_ap(kc, lo=0, hi=512):
            return A_bf[:, kc * 512 + lo: kc * 512 + hi], "A%d" % (kc // 4)

        def sp_ap(c):
            r = 4 + c // 2
            return A[:, r, (c % 2) * 512:(c % 2) * 512 + 512], "A%d" % r

        def e_ap(hh):
            r = 6 + hh % 2
            return A[:, r, 0:512], "A%d" % r

        def einv_ap(hh):
            r = 6 + hh % 2
            return A[:, r, 512:1024], "A%d" % r

        def ysb_ap(s, lo=0, hi=D):
            return A[:, 2 * s:2 * s + 2, :].rearrange("p a b -> p (a b)")[:, lo:hi], ["A%d" % (2 * s), "A%d" % (2 * s + 1)]

        YT_flat = YT[:].rearrange("p a b -> p (a b)")

        def xntok_ap(s):
            return YT_flat[:, s * 2048:(s + 1) * 2048], "YT%d" % s

        def yT_ap(fc, lo=0, hi=512):
            return YT[:, fc, lo:hi], "YT%d" % (fc // 4)

        ALLA = ["A%d" % i for i in range(8)]

        for s in range(n_slots):
            li = s % DEPTH
            P.op("sp", lambda: sp.dma_start(out=WAUG[0:17, :], in_=waug_d[li, :, :]), writes=["WAUG"], dma="waug")
            if s > 0:
                if ring == 1:
                    if s >= DEPTH:
                        P.op("sp", lambda: sp.dma_start(out=S[:].rearrange("p a b -> p (a b)"), in_=st_loc[li, :, 0:1024]),
                             reads=["STL%d" % li], writes=["S0", "S1", "S2", "S3"], dma="sload")
                        P.op("sp", lambda: sp.dma_start(out=UT[:].rearrange("p a b -> p (a b)"), in_=st_loc[li, :, 1024:SW]),
                             reads=["STL%d" % li], writes=["UT"], dma="uload")
                    else:
                        P.op("dve", lambda: dve.memset(S[:].rearrange("p a b -> p (a b)"), 0.0), writes=["S0", "S1", "S2", "S3"])
                        P.op("dve", lambda: dve.memset(UT[:].rearrange("p a b -> p (a b)"), 0.0), writes=["UT"])
                else:
                    XT = T32[:].rearrange("p a b -> p (a b)")
                    for r in range(ring):
                        P.op("sp", lambda r=r: sp.dma_start(out=XT[:, 0:SW], in_=st_all_view(r)),
                             reads=["STALL"], writes=["T32_0", "T32_1"], dma="xt")
                        if r == 0:
                            P.op("dve", lambda: dve.tensor_scalar(out=S[:].rearrange("p a b -> p (a b)"), in0=XT[:, 0:1024],
                                                                  scalar1=OH[:, 0:1], scalar2=None, op0=ALU.mult),
                                 reads=["T32_0", "T32_1", "OH"], writes=["S0", "S1", "S2", "S3"])
                            P.op("dve", lambda: dve.tensor_scalar(out=UT[:].rearrange("p a b -> p (a b)"), in0=XT[:, 1024:SW],
                                                                  scalar1=OH[:, 0:1], scalar2=None, op0=ALU.mult),
                                 reads=["T32_0", "T32_1", "OH"], writes=["UT"])
                        else:
                            P.op("dve", lambda r=r: dve.scalar_tensor_tensor(
                                out=S[:].rearrange("p a b -> p (a b)"), in0=XT[:, 0:1024], scalar=OH[:, r:r + 1],
                                in1=S[:].rearrange("p a b -> p (a b)"), op0=ALU.mult, op1=ALU.add),
                                reads=["T32_0", "T32_1", "OH", "S0", "S1", "S2", "S3"], writes=["S0", "S1", "S2", "S3"])
                            P.op("dve", lambda r=r: dve.scalar_tensor_tensor(
                                out=UT[:].rearrange("p a b -> p (a b)"), in0=XT[:, 1024:SW], scalar=OH[:, r:r + 1],
                                in1=UT[:].rearrange("p a b -> p (a b)"), op0=ALU.mult, op1=ALU.add),
                                reads=["T32_0", "T32_1", "OH", "UT"], writes=["UT"])
            for hh in range(4):
                P.op("act", lambda hh=hh: act.copy(out=SBF[:, hh, :], in_=S[:, hh, :]), reads=["S%d" % hh], writes=["SBF%d" % hh])

            assert ring == 1
            if li == 0:
                for t in range(NSUB):
                    P.op("sp", lambda t=t: sp.dma_start(out=H[:, t, :], in_=xin[s // DEPTH, t * 128:(t + 1) * 128, :]),
                         writes=["H%d" % t], dma="xs%d" % t)

            for t in range(NSUB):
                xa, xk = xntok_ap(t)
                c0 = 0 + t
                P.op("act", lambda t=t, xa=xa, c0=c0: act.activation(out=xa, in_=H[:, t, :], func=AF.Square, accum_out=SSQ[:, c0:c0 + 1]),
                     reads=["H%d" % t], writes=[xk, "SSQa%d" % t])
                P.op("act", lambda c0=c0: act.activation(out=SSQ[:, 4 + c0:5 + c0], in_=SSQ[:, c0:c0 + 1], func=AF.Ln, scale=1.0 / D, bias=EPS),
                     reads=["SSQa%d" % t], writes=["SSQb%d" % t])
                P.op("act", lambda c0=c0: act.activation(out=SSQ[:, 8 + c0:9 + c0], in_=SSQ[:, 4 + c0:5 + c0], func=AF.Exp, scale=-0.5),
                     reads=["SSQb%d" % t], writes=["SSQc%d" % t])
                P.op("dve", lambda t=t, xa=xa, c0=c0: dve.tensor_scalar(out=xa, in0=H[:, t, :], scalar1=SSQ[:, 8 + c0:9 + c0], scalar2=None, op0=ALU.mult),
                     reads=["H%d" % t, "SSQc%d" % t], writes=[xk])
            for kc in range(16):
                b = pbank()
                pt = PS[b][:].bitcast(BF16)
                for t in range(NSUB):
                    xa, xk = xntok_ap(t)
                    P.op("pe", lambda t=t, xa=xa, pt=pt, kc=kc: pe.transpose(pt[:, t * 128:(t + 1) * 128], xa[:, kc * 128:(kc + 1) * 128], IDENT[:]),
                         reads=[xk, "IDENT"], writes=["P%d" % b])
                oa, ok = xnT_ap(kc)
                P.op("act", lambda oa=oa, pt=pt, kc=kc: act.activation(out=oa, in_=pt[:, 0:512], func=AF.Copy, scale=GPRE[:, li * 16 + kc:li * 16 + kc + 1]),
                     reads=["P%d" % b, "GPRE"], writes=[ok])

            XNT_KEYS = ["A0", "A1", "A2", "A3"]

            def proj_B(wi, col_lo, nblk, evac):
                for j in range(nblk):
                    b = pbank()
                    for kc in range(16):
                        ra, rk = xnT_ap(kc)
                        P.op("pe", lambda kc=kc, ra=ra, b=b, j=j: pe.matmul(PS[b][:, :], lhsT=W[wi][:, kc, col_lo + j * 128:col_lo + (j + 1) * 128],
                                                                          rhs=ra, start=(kc == 0), stop=(kc == 15)),
                             reads=[rk, "W%d" % wi], writes=["P%d" % b])
                    evac(j, PS[b], "P%d" % b)

            wi = wload(win_ap(li, Z0 + 512, 528), 528)

            def evac_z(base):
                def f(j, ps, pk):
                    P.op("act", lambda: act.activation(out=ZS[:, base + j, :], in_=ps[:, :], func=AF.Silu), reads=[pk], writes=["ZS%d" % (base + j)])
                return f
            proj_B(wi, 0, 4, evac_z(4))
            b = pbank()
            for kc in range(16):
                ra, rk = xnT_ap(kc)
                P.op("pe", lambda kc=kc, ra=ra, b=b: pe.matmul(PS[b][0:16, :], lhsT=W[wi][:, kc, 512:528], rhs=ra, start=(kc == 0), stop=(kc == 15)),
                     reads=[rk, "W%d" % wi], writes=["P%d" % b])
            P.op("act", lambda b=b: act.copy(out=RTA[0:16, :], in_=PS[b][0:16, :]), reads=["P%d" % b], writes=["RTA"])
            for c in range(NSUB):
                b = pbank()
                P.op("pe", lambda c=c, b=b: pe.matmul(PS[b][:, :], lhsT=RTA[0:17, c * 128:(c + 1) * 128], rhs=WAUG[0:17, :], start=True, stop=True),
                     reads=["RTA", "WAUG"], writes=["P%d" % b])
                sa, sk = sp_ap(c)
                P.op("act", lambda sa=sa, b=b: act.activation(out=sa, in_=PS[b][:, :], func=AF.Exp, scale=-1.0), reads=["P%d" % b], writes=[sk])
                P.op("act", lambda sa=sa: act.activation(out=sa, in_=sa, func=AF.Ln, bias=1.0), reads=[sk], writes=[sk])
            wi = wload(win_ap(li, Z0, 512), 512)
            proj_B(wi, 0, 4, evac_z(0))
            for g in range(2):
                wi = wload(win_ap(li, V0 + g * 512, 512), 512)
                for t in range(NSUB):
                    b = pbank()
                    for kc in range(16):
                        ra, rk = xnT_ap(kc, t * 128, (t + 1) * 128)
                        P.op("pe", lambda kc=kc, ra=ra, b=b, wi=wi: pe.matmul(PS[b][:, :], lhsT=ra, rhs=W[wi][:, kc, 0:512], start=(kc == 0), stop=(kc == 15)),
                             reads=[rk, "W%d" % wi], writes=["P%d" % b])
                    P.op("dve", lambda t=t, g=g, b=b: dve.tensor_copy(out=V[:, t, g * 512:(g + 1) * 512], in_=PS[b][:, :]),
                         reads=["P%d" % b], writes=["V%d" % t])
            wk = wload(win_ap(li, K0, 512), 512)
            wq = wload(win_ap(li, Q0, 512), 512)
            pt6 = PS[6][:].bitcast(BF16)

            def emit_bT(hh):
                bb = pbank()
                for c in range(NSUB):
                    sa, sk = sp_ap(c)
                    P.op("pe", lambda c=c, sa=sa, bb=bb, hh=hh: pe.matmul(PS[bb][:, c * 128:(c + 1) * 128], lhsT=sa[:, hh * 128:(hh + 1) * 128], rhs=TRI, start=True, stop=True),
                         reads=[sk, "CSTF"], writes=["P%d" % bb])
                ea, ek = e_ap(hh)
                ia, ik = einv_ap(hh)
                P.op("act", lambda ea=ea, bb=bb: act.activation(out=ea, in_=PS[bb][:, :], func=AF.Exp), reads=["P%d" % bb], writes=[ek])
                P.op("act", lambda ia=ia, bb=bb: act.activation(out=ia, in_=PS[bb][:, :], func=AF.Exp, scale=-1.0), reads=["P%d" % bb], writes=[ik])
                P.op("dve", lambda ea=ea, hh=hh: dve.tensor_copy(out=DEC[:, hh, :], in_=ea.rearrange("p (c t) -> p c t", t=128)[:, :, 127]), reads=[ek], writes=["DEC%d" % hh])

            def emit_kq(hh):
                ea, ek = e_ap(hh)
                ia, ik = einv_ap(hh)
                b = pbank()
                for kc in range(16):
                    ra, rk = xnT_ap(kc)
                    P.op("pe", lambda kc=kc, ra=ra, b=b, hh=hh: pe.matmul(PS[b][:, :], lhsT=W[wk][:, kc, hh * 128:(hh + 1) * 128], rhs=ra, start=(kc == 0), stop=(kc == 15)),
                         reads=[rk, "W%d" % wk], writes=["P%d" % b])
                P.op("dve", lambda ia=ia, b=b: dve.tensor_tensor(out=T32[:, 0, 0:512], in0=PS[b][:, :], in1=ia, op=ALU.mult), reads=["P%d" % b, ik], writes=["T32_0"])
                for c in range(NSUB):
                    cs = slice(c * 128, (c + 1) * 128)
                    P.op("dve", lambda c=c, cs=cs, ea=ea, hh=hh: dve.tensor_scalar(out=KIN[:, hh, cs], in0=T32[:, 0, cs], scalar1=ea[:, c * 128 + 63:c * 128 + 64], scalar2=None, op0=ALU.mult),
                         reads=["T32_0", ek], writes=["KIN%d" % hh])
                    P.op("dve", lambda c=c, cs=cs, ea=ea, hh=hh: dve.tensor_scalar(out=KSTT[:, hh % 2, c, :], in0=T32[:, 0, cs], scalar1=ea[:, c * 128 + 127:c * 128 + 128], scalar2=None, op0=ALU.mult),
                         reads=["T32_0", ek], writes=["KSTT%d" % (hh % 2)])
                b = pbank()
                for kc in range(16):
                    ra, rk = xnT_ap(kc)
                    P.op("pe", lambda kc=kc, ra=ra, b=b, hh=hh: pe.matmul(PS[b][:, :], lhsT=W[wq][:, kc, hh * 128:(hh + 1) * 128], rhs=ra, start=(kc == 0), stop=(kc == 15)),
                         reads=[rk, "W%d" % wq], writes=["P%d" % b])
                P.op("dve", lambda ea=ea, b=b: dve.scalar_tensor_tensor(out=T32[:, 1, 0:512], in0=PS[b][:, :], scalar=float(128 ** -0.5), in1=ea, op0=ALU.mult, op1=ALU.mult),
                     reads=["P%d" % b, ek], writes=["T32_1"])
                P.op("act", lambda hh=hh: act.copy(out=QB[:, hh, :], in_=T32[:, 1, 0:512]), reads=["T32_1"], writes=["QB%d" % hh])
                for c in range(NSUB):
                    cs = slice(c * 128, (c + 1) * 128)
                    P.op("dve", lambda c=c, cs=cs, ia=ia, hh=hh: dve.tensor_scalar(out=QIN[:, hh, cs], in0=T32[:, 1, cs], scalar1=ia[:, c * 128 + 63:c * 128 + 64], scalar2=None, op0=ALU.mult),
                         reads=["T32_1", ik], writes=["QIN%d" % hh])

            def emit_tr(hh):
                for c in range(NSUB):
                    P.op("pe", lambda c=c, hh=hh: pe.transpose(pt6[:, c * 128:(c + 1) * 128], KSTT[:, hh % 2, c, :], IDENT[:]),
                         reads=["KSTT%d" % (hh % 2), "IDENT"], writes=["P6"])
                P.op("act", lambda hh=hh: act.copy(out=KST[:, :, hh * 128:(hh + 1) * 128], in_=pt6[:, 0:512].rearrange("p (c t) -> p c t", t=128)),
                     reads=["P6"], writes=["KST0", "KST1", "KST2", "KST3"])

            emit_bT(0); emit_bT(1); emit_kq(0); emit_kq(1); emit_tr(0); emit_bT(2); emit_kq(2); emit_tr(1)
            emit_bT(3); emit_kq(3); emit_tr(2); emit_tr(3)

            def gla_gen():
                for hh in range(4):
                    for c in range(NSUB):
                        cs = slice(c * 128, (c + 1) * 128)
                        j = (hh * 4 + c) % 2
                        sc_ps = PS[7][:, j * 128:(j + 1) * 128]
                        P.op("pe", lambda cs=cs, hh=hh, sc_ps=sc_ps: pe.matmul(sc_ps, lhsT=KIN[:, hh, cs], rhs=QIN[:, hh, cs], start=True, stop=True),
                             reads=["KIN%d" % hh, "QIN%d" % hh], writes=["P7"])
                        P.op("dve", lambda j=j, sc_ps=sc_ps: dve.tensor_tensor(out=SCM[:, j, :], in0=sc_ps, in1=MASK[:], op=ALU.mult),
                             reads=["P7", "MASK"], writes=["SCM%d" % j])
                        yield
                        for eh in range(2):
                            ob = 4 + eh
                            P.op("pe", lambda c=c, hh=hh, eh=eh, ob=ob, cs=cs, j=j: pe.matmul(PS[ob][:, cs], lhsT=V[:, c, hh * 256 + eh * 128:hh * 256 + eh * 128 + 128], rhs=SCM[:, j, :], start=True, stop=False),
                                 reads=["V%d" % c, "SCM%d" % j], writes=["P%d" % ob])
                            P.op("pe", lambda hh=hh, eh=eh, ob=ob, cs=cs: pe.matmul(PS[ob][:, cs], lhsT=SBF[:, hh, eh * 128:(eh + 1) * 128], rhs=QB[:, hh, cs], start=False, stop=True),
                                 reads=["SBF%d" % hh, "QB%d" % hh], writes=["P%d" % ob])
                        for eh in range(2):
                            ups = PS[7][:, 256 + eh * 128:256 + (eh + 1) * 128]
                            P.op("pe", lambda c=c, hh=hh, eh=eh, ups=ups: pe.matmul(ups, lhsT=KST[:, c, hh * 128:(hh + 1) * 128], rhs=V[:, c, hh * 256 + eh * 128:hh * 256 + (eh + 1) * 128], start=True, stop=True),
                                 reads=["KST%d" % c, "V%d" % c], writes=["P7"])
                        P.op("dve", lambda c=c, hh=hh: dve.scalar_tensor_tensor(out=S[:, hh, :], in0=S[:, hh, :], scalar=DEC[:, hh, c:c + 1], in1=PS[7][:, 256:512], op0=ALU.mult, op1=ALU.add),
                             reads=["S%d" % hh, "DEC%d" % hh, "P7"], writes=["S%d" % hh])
                        P.op("act", lambda hh=hh: act.copy(out=SBF[:, hh, :], in_=S[:, hh, :]), reads=["S%d" % hh], writes=["SBF%d" % hh])
                        yield
                    for eh in range(2):
                        P.op("act", lambda eh=eh: act.activation(out=SQ[:, eh, :], in_=PS[4 + eh][:, :], func=AF.Square), reads=["P%d" % (4 + eh)], writes=["SQ%d" % eh])
                    yield
                    b = pbank()
                    for eh in range(2):
                        P.op("pe", lambda eh=eh, b=b: pe.matmul(PS[b][:, :], lhsT=ONES[:], rhs=SQ[:, eh, :], start=(eh == 0), stop=(eh == 1)),
                             reads=["ONES", "SQ%d" % eh], writes=["P%d" % b])
                    P.op("act", lambda b=b: act.activation(out=RR[:], in_=PS[b][:, :], func=AF.Ln, scale=1.0 / 256, bias=EPS), reads=["P%d" % b], writes=["RR"])
                    P.op("act", lambda: act.activation(out=RR[:], in_=RR[:], func=AF.Exp, scale=-0.5), reads=["RR"], writes=["RR"])
                    for eh in range(2):
                        fc = hh * 2 + eh
                        P.op("dve", lambda eh=eh, hh=hh: dve.scalar_tensor_tensor(out=YTMP[:, eh, :], in0=PS[4 + eh][:, :], scalar=GOUT[:, li * 2 + eh:li * 2 + eh + 1], in1=RR[:], op0=ALU.mult, op1=ALU.mult),
                             reads=["P%d" % (4 + eh), "GOUT", "RR"], writes=["YTMP%d" % eh])
                        ya, yk = yT_ap(fc)
                        P.op("dve", lambda eh=eh, fc=fc, ya=ya: dve.tensor_tensor(out=ya, in0=YTMP[:, eh, :], in1=ZS[:, fc, :], op=ALU.mult),
                             reads=["YTMP%d" % eh, "ZS%d" % fc], writes=[yk])
                    yield

            def conv_gen():
                for half in range(2):
                    cb0 = half * 4
                    P.op("dve", lambda cb0=cb0: dve.tensor_copy(out=U[:, :, 0:2], in_=UT[:, cb0:cb0 + 4, :]), reads=["UT"], writes=["U0", "U1", "U2", "U3"])

                    def ev_c(j, ps, pk):
                        P.op("act", lambda: act.copy(out=CS[:, j, :], in_=ps[:, :]), reads=[pk], writes=["CS%d" % j])

                    def ev_hc(j, ps, pk, cb0=cb0):
                        cbi = cb0 + j
                        P.op("dve", lambda: dve.tensor_tensor(out=U[:, j, 2:514], in0=ps[:, :], in1=CS[:, j, :], op=ALU.mult), reads=[pk, "CS%d" % j], writes=["U%d" % j])
                        P.op("dve", lambda: dve.tensor_copy(out=UT[:, cbi, :], in_=U[:, j, 512:514]), reads=["U%d" % j], writes=["UT"])
                        w0 = CW[:, li * 24 + cbi * 3 + 0:li * 24 + cbi * 3 + 1]
                        w1 = CW[:, li * 24 + cbi * 3 + 1:li * 24 + cbi * 3 + 2]
                        w2 = CW[:, li * 24 + cbi * 3 + 2:li * 24 + cbi * 3 + 3]
                        P.op("dve", lambda: dve.tensor_scalar(out=CS[:, j, :], in0=U[:, j, 0:512], scalar1=w0, scalar2=None, op0=ALU.mult), reads=["U%d" % j, "CW"], writes=["CS%d" % j])
                        P.op("dve", lambda: dve.scalar_tensor_tensor(out=CS[:, j, :], in0=U[:, j, 1:513], scalar=w1, in1=CS[:, j, :], op0=ALU.mult, op1=ALU.add), reads=["U%d" % j, "CW", "CS%d" % j], writes=["CS%d" % j])
                        P.op("dve", lambda: dve.scalar_tensor_tensor(out=CS[:, j, :], in0=U[:, j, 2:514], scalar=w2, in1=CS[:, j, :], op0=ALU.mult, op1=ALU.add), reads=["U%d" % j, "CW", "CS%d" % j], writes=["CS%d" % j])

                    def ev_b(j, ps, pk):
                        P.op("dve", lambda: dve.tensor_tensor(out=CS[:, j, :], in0=ps[:, :], in1=CS[:, j, :], op=ALU.mult), reads=[pk, "CS%d" % j], writes=["CS%d" % j])

                    def ev_zc(j, ps, pk, cb0=cb0):
                        e = j % 2
                        P.op("act", lambda: act.activation(out=T32[:, e, 0:512], in_=ps[:, :], func=AF.Silu), reads=[pk], writes=["T32_%d" % e])
                        ya, yk = yT_ap(8 + cb0 + j)
                        P.op("dve", lambda: dve.tensor_tensor(out=ya, in0=CS[:, j, :], in1=T32[:, e, 0:512], op=ALU.mult), reads=["CS%d" % j, "T32_%d" % e], writes=[yk])

                    for col0, ev in ((C0, ev_c), (HC0, ev_hc), (B0, ev_b), (ZC0, ev_zc)):
                        wi = wload(win_ap(li, col0 + half * 512, 512), 512)
                        for j in range(4):
                            b = pbank()
                            for kc in range(16):
                                ra, rk = xnT_ap(kc)
                                P.op("pe", lambda kc=kc, ra=ra, b=b, j=j, wi=wi: pe.matmul(PS[b][:, :], lhsT=W[wi][:, kc, j * 128:(j + 1) * 128], rhs=ra, start=(kc == 0), stop=(kc == 15)),
                                     reads=[rk, "W%d" % wi], writes=["P%d" % b])
                            ev(j, PS[b], "P%d" % b)
                            yield

            g1, g2 = gla_gen(), conv_gen()
            live1 = live2 = True
            while live1 or live2:
                if live1:
                    live1 = next(g1, "END") != "END"
                if live2:
                    live2 = next(g2, "END") != "END"

            S_flat = S[:].rearrange("p a b -> p (a b)")
            UT_flat = UT[:].rearrange("p a b -> p (a b)")
            if ring == 1:
                P.op("sp", lambda: sp.dma_start(out=st_loc[li, :, 0:1024], in_=S_flat), reads=["S0", "S1", "S2", "S3"], writes=["STL%d" % li], dma="stl%d" % li)
                P.op("sp", lambda: sp.dma_start(out=st_loc[li, :, 1024:SW], in_=UT_flat), reads=["UT"], writes=["STL%d" % li], dma="stl%d" % li)
            elif s < n_slots - 1:
                P.op("sp", lambda: sp.dma_start(out=st_out[:, 0:1024], in_=S_flat), reads=["S0", "S1", "S2", "S3"], writes=["STOUT"], dma="sto")
                P.op("sp", lambda: sp.dma_start(out=st_out[:, 1024:SW], in_=UT_flat), reads=["UT"], writes=["STOUT"], dma="sto")
                groups = [list(range(g * ring, (g + 1) * ring)) for g in range(n_ranks // ring)]
                P.op("pool", lambda: pool.collective_compute("AllGather", ALU.bypass, groups, [st_out[:, :]], [st_all[0:ring * 128, :]]),
                     reads=["STOUT"], writes=["STALL"], dma="cc")

            P.op("sp", lambda: sp.dma_start(out=GPOST, in_=bass.AP(gpost_d, li * D, [[0, 128], [1, D]])), writes=CSK, dma="gpost")
            for g in range(4):
                wi = wload(wout_ap(li, g), 512)
                for t in range(NSUB):
                    b = pbank()
                    for fc in range(16):
                        ya, yk = yT_ap(fc, t * 128, (t + 1) * 128)
                        P.op("pe", lambda fc=fc, ya=ya, b=b, wi=wi: pe.matmul(PS[b][:, :], lhsT=ya, rhs=W[wi][:, fc, 0:512], start=(fc == 0), stop=(fc == 15)),
                             reads=[yk, "W%d" % wi], writes=["P%d" % b])
                    ya2, yk2 = ysb_ap(t, g * 512, (g + 1) * 512)
                    P.op("act", lambda b=b, ya2=ya2: act.copy(out=ya2, in_=PS[b][:, :]), reads=["P%d" % b], writes=yk2)
                    P.op("act", lambda b=b, t=t, g=g: act.activation(out=SQ[:, 0, :], in_=PS[b][:, :], func=AF.Square, accum_out=SSQ[:, 12 + t * 4 + g:13 + t * 4 + g]),
                         reads=["P%d" % b], writes=["SQ0", "SSQd%d" % t])
            for t in range(NSUB):
                P.op("dve", lambda t=t: dve.tensor_reduce(out=SSQ[:, 28 + t:29 + t], in_=SSQ[:, 12 + t * 4:16 + t * 4], axis=mybir.AxisListType.X, op=ALU.add),
                     reads=["SSQd%d" % t], writes=["SSQe%d" % t])
                P.op("act", lambda t=t: act.activation(out=SSQ[:, 28 + t:29 + t], in_=SSQ[:, 28 + t:29 + t], func=AF.Ln, scale=1.0 / D, bias=EPS), reads=["SSQe%d" % t], writes=["SSQe%d" % t])
                P.op("act", lambda t=t: act.activation(out=SSQ[:, 28 + t:29 + t], in_=SSQ[:, 28 + t:29 + t], func=AF.Exp, scale=-0.5), reads=["SSQe%d" % t], writes=["SSQe%d" % t])
                ya, yk = ysb_ap(t)
                P.op("dve", lambda t=t, ya=ya: dve.scalar_tensor_tensor(out=ya, in0=ya, scalar=SSQ[:, 28 + t:29 + t], in1=GPOST, op0=ALU.mult, op1=ALU.mult),
                     reads=yk + ["SSQe%d" % t] + CSK, writes=yk)
                P.op("dve", lambda t=t, ya=ya: dve.tensor_tensor(out=H[:, t, :], in0=H[:, t, :], in1=ya, op=ALU.add), reads=yk + ["H%d" % t], writes=["H%d" % t])
                if li == DEPTH - 1 and s // DEPTH >= 1:
                    P.op("sp", lambda t=t: sp.dma_start(out=yout[s // DEPTH - 1, t * 128:(t + 1) * 128, :], in_=H[:, t, :]), reads=["H%d" % t], writes=["YOUT"], dma="yo%d" % t)

        P.wait_all("sp")
    return nc


def _consts():
    ident = np.eye(128, dtype=np.float32)
    jj, ii = np.meshgrid(np.arange(128), np.arange(128), indexing="ij")
    mask = (jj <= ii).astype(np.float32)
    tri = mask * np.float32(-1.0 / 16.0)
    ones = np.ones((128, 128), np.float32)
    return np.ascontiguousarray(np.concatenate([ident, mask, tri, ones], axis=1))


def _layout_params(norm_pre, w_gate_up, b_gate, gla_out_norm, conv_w, norm_post, rot):
    order = [(i - rot) % DEPTH for i in range(DEPTH)]
    gpre = np.concatenate([norm_pre[l].reshape(16, 128).T for l in order], axis=1)
    gout = np.concatenate([gla_out_norm[l].reshape(2, 128).T for l in order], axis=1)
    cw = np.concatenate([conv_w[l].reshape(3, 8, 128).transpose(2, 1, 0).reshape(128, 24) for l in order], axis=1)
    waug = np.stack([np.concatenate([w_gate_up[l], b_gate[l][None, :]], axis=0) for l in order], axis=0)
    gpost = np.stack([norm_post[l] for l in order], axis=0)
    f = lambda a: np.ascontiguousarray(a, dtype=np.float32)
    return f(gpre), f(gout), f(cw), f(waug), f(gpost), order


def run_ring1(x, meta_tokens, norm_pre, w_in, w_gate_up, b_gate, gla_out_norm, conv_w, w_out, norm_post, n_tok_tiles):
    x = np.asarray(x, np.float32)
    meta_tokens = np.asarray(meta_tokens, np.float32)
    B = x.shape[0]
    n_tiles = 1 + n_tok_tiles
    n_cores = B
    n_slots = n_tiles * DEPTH
    nc = build_program(n_slots, 1, n_cores, n_tiles)
    cst = _consts()
    gpre, gout, cw, waug, gpost, order = _layout_params(norm_pre, w_gate_up, b_gate, gla_out_norm, conv_w, norm_post, 0)
    w_in = np.asarray(w_in, np.float32)
    w_out = np.asarray(w_out, np.float32)
    in_maps = []
    for c in range(n_cores):
        xin = np.zeros((n_tiles, T, D), np.float32)
        xin[0, T - NMETA:, :] = meta_tokens
        xin[1:] = x[c].reshape(n_tok_tiles, T, D)
        keep = np.ones((128, 2 * n_slots), np.float32)
        oh = np.zeros((128, 8), np.float32)
        mp = dict(xin=xin, waug=waug, gpre=gpre, gpost=gpost, gout=gout, cw=cw, keep=keep, onehot=oh, cst=cst)
        for l in range(DEPTH):
            mp["w_in%d" % l] = np.ascontiguousarray(w_in[l])
            mp["w_out%d" % l] = np.ascontiguousarray(w_out[l])
        in_maps.append(mp)
    res = run_bass_kernel_spmd(nc, in_maps, core_ids=list(range(n_cores)))
    out = np.empty((B, n_tok_tiles * T, D), np.float32)
    for c in range(n_cores):
        out[c] = res.results[c]["yout"].reshape(n_tok_tiles * T, D)
    return out, res


def run_ring(x, meta_tokens, norm_pre, w_in, w_gate_up, b_gate, gla_out_norm, conv_w, w_out, norm_post, n_tok_tiles, ring=4):
    x = np.asarray(x, np.float32)
    meta_tokens = np.asarray(meta_tokens, np.float32)
    B = x.shape[0]
    n_tiles = 1 + n_tok_tiles
    n_cores = B * ring
    n_slots = n_tiles + ring - 1
    n_xt = (n_tiles + ring - 1) // ring
    nc = build_program(n_slots, ring, n_cores, n_xt)
    cst = _consts()
    w_in = np.asarray(w_in, np.float32)
    w_out = np.asarray(w_out, np.float32)
    per_rp = []
    for rp in range(ring):
        gpre, gout, cw, waug, gpost, order = _layout_params(norm_pre, w_gate_up, b_gate, gla_out_norm, conv_w, norm_post, rp)
        keep = np.zeros((128, 2 * n_slots), np.float32)
        for s in range(n_slots):
            m = s - rp
            if m >= 0 and m % DEPTH != 0:
                keep[:, s] = 1.0
            if m >= 0 and m % DEPTH == 0 and (DEPTH * 0 + ring * (m // DEPTH) + rp) < n_tiles:
                keep[:, n_slots + s] = 1.0
        oh = np.zeros((128, 8), np.float32)
        oh[:, (rp - 1) % ring] = 1.0
        mp = dict(waug=waug, gpre=gpre, gpost=gpost, gout=gout, cw=cw, keep=keep, onehot=oh, cst=cst)
        for l in range(DEPTH):
            mp["w_in%d" % l] = np.ascontiguousarray(w_in[order[l]])
            mp["w_out%d" % l] = np.ascontiguousarray(w_out[order[l]])
        per_rp.append(mp)
    in_maps = []
    for c in range(n_cores):
        b, rp = c // ring, c % ring
        xin = np.zeros((n_xt, T, D), np.float32)
        for i in range(n_xt):
            j = ring * i + rp
            if j == 0:
                xin[i, T - NMETA:, :] = meta_tokens
            elif j < n_tiles:
                xin[i] = x[b, (j - 1) * T:j * T, :]
        mp = dict(per_rp[rp])
        mp["xin"] = xin
        in_maps.append(mp)
    res = run_bass_kernel_spmd(nc, in_maps, core_ids=list(range(n_cores)))
    out = np.empty((B, n_tok_tiles * T, D), np.float32)
    for c in range(n_cores):
        b, rp = c // ring, c % ring
        yo = res.results[c]["yout"]
        for i in range(n_xt):
            j = ring * i + rp
            if 1 <= j < n_tiles:
                out[b, (j - 1) * T:j * T, :] = yo[DEPTH * i + rp]
    return out, res


def kernel(x, meta_tokens, norm_pre, w_in, w_gate_up, b_gate, gla_out_norm, conv_w, w_out, norm_post):
    out, _ = run_ring1(x, meta_tokens, norm_pre, w_in, w_gate_up, b_gate, gla_out_norm, conv_w, w_out, norm_post, SEQ // T)
    return out
```

```python
import contextlib
import numpy as np
import concourse.bass as bass
import concourse.mybir as mybir
from concourse.bass_utils import run_bass_kernel_spmd

F32 = mybir.dt.float32
BF16 = mybir.dt.bfloat16
AF = mybir.ActivationFunctionType
ALU = mybir.AluOpType

D = 2048
DP = 7184
T = 512
NSUB = 4
DEPTH = 4
NMETA = 16
SEQ = 8192
EPS = 1e-6
Q0, K0, V0, Z0, R0, HC0, B0, C0, ZC0 = 0, 512, 1024, 2048, 3072, 3088, 4112, 5136, 6160
SW = 1040


class Prog:
    def __init__(self, nc, es):
        self.nc = nc
        self.es = es
        self.eng = {}
        for name, h in (("pe", nc.tensor), ("act", nc.scalar), ("dve", nc.vector),
                        ("pool", nc.gpsimd), ("sp", nc.sync)):
            self.eng[name] = dict(h=h, sem=es.enter_context(nc.semaphore("s_" + name)), n=0)
        self.waited = {}
        self.lastw = {}
        self.readers = {}
        self.dsem = {}
        self.nwait = 0

    def dma_sem(self, name):
        if name not in self.dsem:
            self.dsem[name] = [self.es.enter_context(self.nc.semaphore("d_" + name)), 0]
        return self.dsem[name]

    def _wait(self, ename, tok):
        sem, val, src = tok
        if src == ename and ename in ("pe", "sp"):
            return
        k = (ename, id(sem))
        if self.waited.get(k, 0) >= val:
            return
        self.waited[k] = val
        self.eng[ename]["h"].wait_ge(sem, val)
        self.nwait += 1

    def op(self, ename, fn, reads=(), writes=(), dma=None, same_war=False):
        e = self.eng[ename]
        deps = []
        for k in reads:
            if k in self.lastw:
                deps.append(self.lastw[k])
        for k in writes:
            if k in self.lastw:
                deps.append(self.lastw[k])
            for r in self.readers.get(k, ()):
                if r[2] == ename and dma is None and ename != "pool":
                    continue
                deps.append(r)
        for tok in deps:
            self._wait(ename, tok)
        ins = fn()
        if dma is None:
            e["n"] += 1
            ins.then_inc(e["sem"], 1)
            tok = (e["sem"], e["n"], ename)
        else:
            ds = self.dma_sem(dma)
            ds[1] += 16
            ins.then_inc(ds[0], 16)
            tok = (ds[0], ds[1], "dma:" + dma)
        for k in writes:
            self.lastw[k] = tok
            self.readers[k] = []
        for k in reads:
            self.readers.setdefault(k, []).append(tok)
        return tok

    def wait_all(self, ename):
        for k, tok in list(self.lastw.items()):
            self._wait(ename, tok)
        for k, rs in list(self.readers.items()):
            for r in rs:
                self._wait(ename, r)


def build_program(n_slots, ring, n_ranks, n_xt=None):
    nc = bass.Bass("TRN2", target_bir_lowering=False)
    dt = nc.dram_tensor
    if n_xt is None:
        n_xt = n_slots
    xin = dt("xin", [n_xt, T, D], F32, kind="ExternalInput").ap()
    w_in = [dt("w_in%d" % l, [D, DP], F32, kind="ExternalInput").ap() for l in range(DEPTH)]
    w_out = [dt("w_out%d" % l, [D, D], F32, kind="ExternalInput").ap() for l in range(DEPTH)]
    waug_d = dt("waug", [DEPTH, 17, 512], F32, kind="ExternalInput").ap()
    gpre_d = dt("gpre", [128, DEPTH * 16], F32, kind="ExternalInput").ap()
    gpost_d = dt("gpost", [DEPTH, D], F32, kind="ExternalInput")
    gout_d = dt("gout", [128, DEPTH * 2], F32, kind="ExternalInput").ap()
    cw_d = dt("cw", [128, DEPTH * 24], F32, kind="ExternalInput").ap()
    keep_d = dt("keep", [128, 2 * n_slots], F32, kind="ExternalInput").ap()
    oh_d = dt("onehot", [128, 8], F32, kind="ExternalInput").ap()
    cst_d = dt("cst", [128, 4 * 128], F32, kind="ExternalInput").ap()
    yout = dt("yout", [n_slots // DEPTH - 1, T, D], F32, kind="ExternalOutput").ap()
    if ring > 1:
        st_out = dt("st_out", [128, SW], F32).ap()
        st_all = dt("st_all", [ring * 128, SW], F32).ap()

        def st_all_view(r):
            return st_all[r * 128:(r + 1) * 128, :]
    else:
        st_loc = dt("st_loc", [DEPTH, 128, SW], F32).ap()

    es = contextlib.ExitStack()
    with es:
        P = Prog(nc, es)
        sb = lambda name, shape, d: es.enter_context(nc.sbuf_tensor(name, shape, d))
        H = sb("H", [128, NSUB, D], F32)
        A = sb("A", [128, 8, 1024], F32)
        YT = sb("YT", [128, 16, 512], BF16)
        W = [sb("W0", [128, 16, 528], BF16), sb("W1", [128, 16, 528], BF16), sb("W2", [128, 16, 528], BF16)]
        ZS = sb("ZS", [128, 8, 512], BF16)
        V = sb("V", [128, NSUB, 1024], BF16)
        QB = sb("QB", [128, 4, 512], BF16)
        QIN = sb("QIN", [128, 4, 512], BF16)
        KIN = sb("KIN", [128, 4, 512], BF16)
        KST = sb("KST", [128, NSUB, 512], BF16)
        KSTT = sb("KSTT", [128, 2, 4, 128], BF16)
        YTMP = sb("YTMP", [128, 2, 512], F32)
        T32 = sb("T32", [128, 2, 520], F32)
        CS = sb("CS", [128, 4, 512], F32)
        U = sb("U", [128, 4, 514], F32)
        S = sb("S", [128, 4, 256], F32)
        SBF = sb("SBF", [128, 4, 256], BF16)
        UT = sb("UT", [128, 8, 2], F32)
        DEC = sb("DEC", [128, 4, 4], F32)
        SCM = sb("SCM", [128, 2, 128], BF16)
        SQ = sb("SQ", [128, 2, 512], BF16)
        RR = sb("RR", [128, 512], F32)
        WAUG = sb("WAUG", [32, 512], F32)
        RTA = sb("RTA", [32, 512], F32)
        GPRE = sb("GPRE", [128, DEPTH * 16], F32)
        GOUT = sb("GOUT", [128, DEPTH * 2], F32)
        CW = sb("CW", [128, DEPTH * 24], F32)
        KEEP = sb("KEEP", [128, 2 * n_slots], F32)
        OH = sb("OH", [128, 8], F32)
        CSTF = sb("CSTF", [128, 4 * 128], F32)
        GPOST = CS[:].rearrange("p a b -> p (a b)")
        CSK = ["CS0", "CS1", "CS2", "CS3"]
        IDENT = sb("IDENT", [128, 128], BF16)
        MASK = sb("MASK", [128, 128], BF16)
        ONES = sb("ONES", [128, 128], BF16)
        SSQ = sb("SSQ", [128, 32], F32)
        PS = [es.enter_context(nc.psum_tensor("PS%d" % i, [128, 512], F32)) for i in range(8)]
        es.enter_context(nc.Block())

        act, dve, pe, pool, sp = nc.scalar, nc.vector, nc.tensor, nc.gpsimd, nc.sync
        TRI = CSTF[:, 256:384]

        P.op("sp", lambda: sp.dma_start(out=CSTF[:], in_=cst_d[:, :]), writes=["CSTF"], dma="c0")
        P.op("sp", lambda: sp.dma_start(out=GPRE[:], in_=gpre_d[:, :]), writes=["GPRE"], dma="c1")
        P.op("sp", lambda: sp.dma_start(out=GOUT[:], in_=gout_d[:, :]), writes=["GOUT"], dma="c2")
        P.op("sp", lambda: sp.dma_start(out=CW[:], in_=cw_d[:, :]), writes=["CW"], dma="c3")
        P.op("sp", lambda: sp.dma_start(out=KEEP[:], in_=keep_d[:, :]), writes=["KEEP"], dma="c4")
        P.op("sp", lambda: sp.dma_start(out=OH[:], in_=oh_d[:, :]), writes=["OH"], dma="c5")
        P.op("dve", lambda: dve.tensor_copy(out=IDENT[:], in_=CSTF[:, 0:128]), reads=["CSTF"], writes=["IDENT"])
        P.op("dve", lambda: dve.tensor_copy(out=MASK[:], in_=CSTF[:, 128:256]), reads=["CSTF"], writes=["MASK"])
        P.op("dve", lambda: dve.tensor_copy(out=ONES[:], in_=CSTF[:, 384:512]), reads=["CSTF"], writes=["ONES"])
        P.op("dve", lambda: dve.memset(RTA[:], 1.0), writes=["RTA"])
        P.op("dve", lambda: dve.memset(H[:].rearrange("p a b -> p (a b)"), 0.0), writes=["H0", "H1", "H2", "H3"])
        P.op("dve", lambda: dve.memset(S[:].rearrange("p a b -> p (a b)"), 0.0), writes=["S0", "S1", "S2", "S3"])
        P.op("dve", lambda: dve.memset(UT[:].rearrange("p a b -> p (a b)"), 0.0), writes=["UT"])
        wstate = dict(n=0)

        def wload(dram_ap, ncols):
            i = wstate["n"] % len(W)
            wstate["n"] += 1
            P.op("pool", lambda: pool.dma_start(out=W[i][:, :, 0:ncols], in_=dram_ap),
                 writes=["W%d" % i], dma="w%d" % i)
            return i

        def win_ap(li, c0, ncols):
            return w_in[li][:, c0:c0 + ncols].rearrange("(kc p) c -> p kc c", p=128)

        def wout_ap(li, g):
            return w_out[li][:, g * 512:(g + 1) * 512].rearrange("(kc p) c -> p kc c", p=128)

        pstate = dict(n=0)

        def pbank():
            i = pstate["n"] % 4
            pstate["n"] += 1
            return i

        A_bf = A[:].rearrange("p a b -> p (a b)").bitcast(BF16)

        def xnT_ap(kc, lo=0, hi=512):
            return A_bf[:, kc * 512 + lo: kc * 512 + hi], "A%d" % (kc // 4)

        def sp_ap(c):
            r = 4 + c // 2
            return A[:, r, (c % 2) * 512:(c % 2) * 512 + 512], "A%d" % r

        def e_ap(hh):
            r = 6 + hh % 2
            return A[:, r, 0:512], "A%d" % r

        def einv_ap(hh):
            r = 6 + hh % 2
            return A[:, r, 512:1024], "A%d" % r

        def ysb_ap(s, lo=0, hi=D):
            return A[:, 2 * s:2 * s + 2, :].rearrange("p a b -> p (a b)")[:, lo:hi], ["A%d" % (2 * s), "A%d" % (2 * s + 1)]

        YT_flat = YT[:].rearrange("p a b -> p (a b)")

        def xntok_ap(s):
            return YT_flat[:, s * 2048:(s + 1) * 2048], "YT%d" % s

        def yT_ap(fc, lo=0, hi=512):
            return YT[:, fc, lo:hi], "YT%d" % (fc // 4)

        ALLA = ["A%d" % i for i in range(8)]

        for s in range(n_slots):
            li = s % DEPTH
            P.op("sp", lambda: sp.dma_start(out=WAUG[0:17, :], in_=waug_d[li, :, :]), writes=["WAUG"], dma="waug")
            if s > 0:
                if ring == 1:
                    if s >= DEPTH:
                        P.op("sp", lambda: sp.dma_start(out=S[:].rearrange("p a b -> p (a b)"), in_=st_loc[li, :, 0:1024]),
                             reads=["STL%d" % li], writes=["S0", "S1", "S2", "S3"], dma="sload")
                        P.op("sp", lambda: sp.dma_start(out=UT[:].rearrange("p a b -> p (a b)"), in_=st_loc[li, :, 1024:SW]),
                             reads=["STL%d" % li], writes=["UT"], dma="uload")
                    else:
                        P.op("dve", lambda: dve.memset(S[:].rearrange("p a b -> p (a b)"), 0.0), writes=["S0", "S1", "S2", "S3"])
                        P.op("dve", lambda: dve.memset(UT[:].rearrange("p a b -> p (a b)"), 0.0), writes=["UT"])
                else:
                    XT = T32[:].rearrange("p a b -> p (a b)")
                    for r in range(ring):
                        P.op("sp", lambda r=r: sp.dma_start(out=XT[:, 0:SW], in_=st_all_view(r)),
                             reads=["STALL"], writes=["T32_0", "T32_1"], dma="xt")
                        if r == 0:
                            P.op("dve", lambda: dve.tensor_scalar(out=S[:].rearrange("p a b -> p (a b)"), in0=XT[:, 0:1024],
                                                                  scalar1=OH[:, 0:1], scalar2=None, op0=ALU.mult),
                                 reads=["T32_0", "T32_1", "OH"], writes=["S0", "S1", "S2", "S3"])
                            P.op("dve", lambda: dve.tensor_scalar(out=UT[:].rearrange("p a b -> p (a b)"), in0=XT[:, 1024:SW],
                                                                  scalar1=OH[:, 0:1], scalar2=None, op0=ALU.mult),
                                 reads=["T32_0", "T32_1", "OH"], writes=["UT"])
                        else:
                            P.op("dve", lambda r=r: dve.scalar_tensor_tensor(
                                out=S[:].rearrange("p a b -> p (a b)"), in0=XT[:, 0:1024], scalar=OH[:, r:r + 1],
                                in1=S[:].rearrange("p a b -> p (a b)"), op0=ALU.mult, op1=ALU.add),
                                reads=["T32_0", "T32_1", "OH", "S0", "S1", "S2", "S3"], writes=["S0", "S1", "S2", "S3"])
                            P.op("dve", lambda r=r: dve.scalar_tensor_tensor(
                                out=UT[:].rearrange("p a b -> p (a b)"), in0=XT[:, 1024:SW], scalar=OH[:, r:r + 1],
                                in1=UT[:].rearrange("p a b -> p (a b)"), op0=ALU.mult, op1=ALU.add),
                                reads=["T32_0", "T32_1", "OH", "UT"], writes=["UT"])
            for hh in range(4):
                P.op("act", lambda hh=hh: act.copy(out=SBF[:, hh, :], in_=S[:, hh, :]), reads=["S%d" % hh], writes=["SBF%d" % hh])

            assert ring == 1
            if li == 0:
                for t in range(NSUB):
                    P.op("sp", lambda t=t: sp.dma_start(out=H[:, t, :], in_=xin[s // DEPTH, t * 128:(t + 1) * 128, :]),
                         writes=["H%d" % t], dma="xs%d" % t)

            for t in range(NSUB):
                xa, xk = xntok_ap(t)
                c0 = 0 + t
                P.op("act", lambda t=t, xa=xa, c0=c0: act.activation(out=xa, in_=H[:, t, :], func=AF.Square, accum_out=SSQ[:, c0:c0 + 1]),
                     reads=["H%d" % t], writes=[xk, "SSQa%d" % t])
                P.op("act", lambda c0=c0: act.activation(out=SSQ[:, 4 + c0:5 + c0], in_=SSQ[:, c0:c0 + 1], func=AF.Ln, scale=1.0 / D, bias=EPS),
                     reads=["SSQa%d" % t], writes=["SSQb%d" % t])
                P.op("act", lambda c0=c0: act.activation(out=SSQ[:, 8 + c0:9 + c0], in_=SSQ[:, 4 + c0:5 + c0], func=AF.Exp, scale=-0.5),
                     reads=["SSQb%d" % t], writes=["SSQc%d" % t])
                P.op("dve", lambda t=t, xa=xa, c0=c0: dve.tensor_scalar(out=xa, in0=H[:, t, :], scalar1=SSQ[:, 8 + c0:9 + c0], scalar2=None, op0=ALU.mult),
                     reads=["H%d" % t, "SSQc%d" % t], writes=[xk])
            for kc in range(16):
                b = pbank()
                pt = PS[b][:].bitcast(BF16)
                for t in range(NSUB):
                    xa, xk = xntok_ap(t)
                    P.op("pe", lambda t=t, xa=xa, pt=pt, kc=kc: pe.transpose(pt[:, t * 128:(t + 1) * 128], xa[:, kc * 128:(kc + 1) * 128], IDENT[:]),
                         reads=[xk, "IDENT"], writes=["P%d" % b])
                oa, ok = xnT_ap(kc)
                P.op("act", lambda oa=oa, pt=pt, kc=kc: act.activation(out=oa, in_=pt[:, 0:512], func=AF.Copy, scale=GPRE[:, li * 16 + kc:li * 16 + kc + 1]),
                     reads=["P%d" % b, "GPRE"], writes=[ok])

            XNT_KEYS = ["A0", "A1", "A2", "A3"]

            def proj_B(wi, col_lo, nblk, evac):
                for j in range(nblk):
                    b = pbank()
                    for kc in range(16):
                        ra, rk = xnT_ap(kc)
                        P.op("pe", lambda kc=kc, ra=ra, b=b, j=j: pe.matmul(PS[b][:, :], lhsT=W[wi][:, kc, col_lo + j * 128:col_lo + (j + 1) * 128],
                                                                          rhs=ra, start=(kc == 0), stop=(kc == 15)),
                             reads=[rk, "W%d" % wi], writes=["P%d" % b])
                    evac(j, PS[b], "P%d" % b)

            wi = wload(win_ap(li, Z0 + 512, 528), 528)

            def evac_z(base):
                def f(j, ps, pk):
                    P.op("act", lambda: act.activation(out=ZS[:, base + j, :], in_=ps[:, :], func=AF.Silu), reads=[pk], writes=["ZS%d" % (base + j)])
                return f
            proj_B(wi, 0, 4, evac_z(4))
            b = pbank()
            for kc in range(16):
                ra, rk = xnT_ap(kc)
                P.op("pe", lambda kc=kc, ra=ra, b=b: pe.matmul(PS[b][0:16, :], lhsT=W[wi][:, kc, 512:528], rhs=ra, start=(kc == 0), stop=(kc == 15)),
                     reads=[rk, "W%d" % wi], writes=["P%d" % b])
            P.op("act", lambda b=b: act.copy(out=RTA[0:16, :], in_=PS[b][0:16, :]), reads=["P%d" % b], writes=["RTA"])
            for c in range(NSUB):
                b = pbank()
                P.op("pe", lambda c=c, b=b: pe.matmul(PS[b][:, :], lhsT=RTA[0:17, c * 128:(c + 1) * 128], rhs=WAUG[0:17, :], start=True, stop=True),
                     reads=["RTA", "WAUG"], writes=["P%d" % b])
                sa, sk = sp_ap(c)
                P.op("act", lambda sa=sa, b=b: act.activation(out=sa, in_=PS[b][:, :], func=AF.Exp, scale=-1.0), reads=["P%d" % b], writes=[sk])
                P.op("act", lambda sa=sa: act.activation(out=sa, in_=sa, func=AF.Ln, bias=1.0), reads=[sk], writes=[sk])
            wi = wload(win_ap(li, Z0, 512), 512)
            proj_B(wi, 0, 4, evac_z(0))
            for g in range(2):
                wi = wload(win_ap(li, V0 + g * 512, 512), 512)
                for t in range(NSUB):
                    b = pbank()
                    for kc in range(16):
                        ra, rk = xnT_ap(kc, t * 128, (t + 1) * 128)
                        P.op("pe", lambda kc=kc, ra=ra, b=b, wi=wi: pe.matmul(PS[b][:, :], lhsT=ra, rhs=W[wi][:, kc, 0:512], start=(kc == 0), stop=(kc == 15)),
                             reads=[rk, "W%d" % wi], writes=["P%d" % b])
                    P.op("dve", lambda t=t, g=g, b=b: dve.tensor_copy(out=V[:, t, g * 512:(g + 1) * 512], in_=PS[b][:, :]),
                         reads=["P%d" % b], writes=["V%d" % t])
            wk = wload(win_ap(li, K0, 512), 512)
            wq = wload(win_ap(li, Q0, 512), 512)
            pt6 = PS[6][:].bitcast(BF16)

            def emit_bT(hh):
                bb = pbank()
                for c in range(NSUB):
                    sa, sk = sp_ap(c)
                    P.op("pe", lambda c=c, sa=sa, bb=bb, hh=hh: pe.matmul(PS[bb][:, c * 128:(c + 1) * 128], lhsT=sa[:, hh * 128:(hh + 1) * 128], rhs=TRI, start=True, stop=True),
                         reads=[sk, "CSTF"], writes=["P%d" % bb])
                ea, ek = e_ap(hh)
                ia, ik = einv_ap(hh)
                P.op("act", lambda ea=ea, bb=bb: act.activation(out=ea, in_=PS[bb][:, :], func=AF.Exp), reads=["P%d" % bb], writes=[ek])
                P.op("act", lambda ia=ia, bb=bb: act.activation(out=ia, in_=PS[bb][:, :], func=AF.Exp, scale=-1.0), reads=["P%d" % bb], writes=[ik])
                P.op("dve", lambda ea=ea, hh=hh: dve.tensor_copy(out=DEC[:, hh, :], in_=ea.rearrange("p (c t) -> p c t", t=128)[:, :, 127]), reads=[ek], writes=["DEC%d" % hh])

            def emit_kq(hh):
                ea, ek = e_ap(hh)
                ia, ik = einv_ap(hh)
                b = pbank()
                for kc in range(16):
                    ra, rk = xnT_ap(kc)
                    P.op("pe", lambda kc=kc, ra=ra, b=b, hh=hh: pe.matmul(PS[b][:, :], lhsT=W[wk][:, kc, hh * 128:(hh + 1) * 128], rhs=ra, start=(kc == 0), stop=(kc == 15)),
                         reads=[rk, "W%d" % wk], writes=["P%d" % b])
                P.op("dve", lambda ia=ia, b=b: dve.tensor_tensor(out=T32[:, 0, 0:512], in0=PS[b][:, :], in1=ia, op=ALU.mult), reads=["P%d" % b, ik], writes=["T32_0"])
                for c in range(NSUB):
                    cs = slice(c * 128, (c + 1) * 128)
                    P.op("dve", lambda c=c, cs=cs, ea=ea, hh=hh: dve.tensor_scalar(out=KIN[:, hh, cs], in0=T32[:, 0, cs], scalar1=ea[:, c * 128 + 63:c * 128 + 64], scalar2=None, op0=ALU.mult),
                         reads=["T32_0", ek], writes=["KIN%d" % hh])
                    P.op("dve", lambda c=c, cs=cs, ea=ea, hh=hh: dve.tensor_scalar(out=KSTT[:, hh % 2, c, :], in0=T32[:, 0, cs], scalar1=ea[:, c * 128 + 127:c * 128 + 128], scalar2=None, op0=ALU.mult),
                         reads=["T32_0", ek], writes=["KSTT%d" % (hh % 2)])
                b = pbank()
                for kc in range(16):
                    ra, rk = xnT_ap(kc)
                    P.op("pe", lambda kc=kc, ra=ra, b=b, hh=hh: pe.matmul(PS[b][:, :], lhsT=W[wq][:, kc, hh * 128:(hh + 1) * 128], rhs=ra, start=(kc == 0), stop=(kc == 15)),
                         reads=[rk, "W%d" % wq], writes=["P%d" % b])
                P.op("dve", lambda ea=ea, b=b: dve.scalar_tensor_tensor(out=T32[:, 1, 0:512], in0=PS[b][:, :], scalar=float(128 ** -0.5), in1=ea, op0=ALU.mult, op1=ALU.mult),
                     reads=["P%d" % b, ek], writes=["T32_1"])
                P.op("act", lambda hh=hh: act.copy(out=QB[:, hh, :], in_=T32[:, 1, 0:512]), reads=["T32_1"], writes=["QB%d" % hh])
                for c in range(NSUB):
                    cs = slice(c * 128, (c + 1) * 128)
                    P.op("dve", lambda c=c, cs=cs, ia=ia, hh=hh: dve.tensor_scalar(out=QIN[:, hh, cs], in0=T32[:, 1, cs], scalar1=ia[:, c * 128 + 63:c * 128 + 64], scalar2=None, op0=ALU.mult),
                         reads=["T32_1", ik], writes=["QIN%d" % hh])

            def emit_tr(hh):
                for c in range(NSUB):
                    P.op("pe", lambda c=c, hh=hh: pe.transpose(pt6[:, c * 128:(c + 1) * 128], KSTT[:, hh % 2, c, :], IDENT[:]),
                         reads=["KSTT%d" % (hh % 2), "IDENT"], writes=["P6"])
                P.op("act", lambda hh=hh: act.copy(out=KST[:, :, hh * 128:(hh + 1) * 128], in_=pt6[:, 0:512].rearrange("p (c t) -> p c t", t=128)),
                     reads=["P6"], writes=["KST0", "KST1", "KST2", "KST3"])

            emit_bT(0); emit_bT(1); emit_kq(0); emit_kq(1); emit_tr(0); emit_bT(2); emit_kq(2); emit_tr(1)
            emit_bT(3); emit_kq(3); emit_tr(2); emit_tr(3)

            def gla_gen():
                for hh in range(4):
                    for c in range(NSUB):
                        cs = slice(c * 128, (c + 1) * 128)
                        j = (hh * 4 + c) % 2
                        sc_ps = PS[7][:, j * 128:(j + 1) * 128]
                        P.op("pe", lambda cs=cs, hh=hh, sc_ps=sc_ps: pe.matmul(sc_ps, lhsT=KIN[:, hh, cs], rhs=QIN[:, hh, cs], start=True, stop=True),
                             reads=["KIN%d" % hh, "QIN%d" % hh], writes=["P7"])
                        P.op("dve", lambda j=j, sc_ps=sc_ps: dve.tensor_tensor(out=SCM[:, j, :], in0=sc_ps, in1=MASK[:], op=ALU.mult),
                             reads=["P7", "MASK"], writes=["SCM%d" % j])
                        yield
                        for eh in range(2):
                            ob = 4 + eh
                            P.op("pe", lambda c=c, hh=hh, eh=eh, ob=ob, cs=cs, j=j: pe.matmul(PS[ob][:, cs], lhsT=V[:, c, hh * 256 + eh * 128:hh * 256 + eh * 128 + 128], rhs=SCM[:, j, :], start=True, stop=False),
                                 reads=["V%d" % c, "SCM%d" % j], writes=["P%d" % ob])
                            P.op("pe", lambda hh=hh, eh=eh, ob=ob, cs=cs: pe.matmul(PS[ob][:, cs], lhsT=SBF[:, hh, eh * 128:(eh + 1) * 128], rhs=QB[:, hh, cs], start=False, stop=True),
                                 reads=["SBF%d" % hh, "QB%d" % hh], writes=["P%d" % ob])
                        for eh in range(2):
                            ups = PS[7][:, 256 + eh * 128:256 + (eh + 1) * 128]
                            P.op("pe", lambda c=c, hh=hh, eh=eh, ups=ups: pe.matmul(ups, lhsT=KST[:, c, hh * 128:(hh + 1) * 128], rhs=V[:, c, hh * 256 + eh * 128:hh * 256 + (eh + 1) * 128], start=True, stop=True),
                                 reads=["KST%d" % c, "V%d" % c], writes=["P7"])
                        P.op("dve", lambda c=c, hh=hh: dve.scalar_tensor_tensor(out=S[:, hh, :], in0=S[:, hh, :], scalar=DEC[:, hh, c:c + 1], in1=PS[7][:, 256:512], op0=ALU.mult, op1=ALU.add),
                             reads=["S%d" % hh, "DEC%d" % hh, "P7"], writes=["S%d" % hh])
                        P.op("act", lambda hh=hh: act.copy(out=SBF[:, hh, :], in_=S[:, hh, :]), reads=["S%d" % hh], writes=["SBF%d" % hh])
                        yield
                    for eh in range(2):
                        P.op("act", lambda eh=eh: act.activation(out=SQ[:, eh, :], in_=PS[4 + eh][:, :], func=AF.Square), reads=["P%d" % (4 + eh)], writes=["SQ%d" % eh])
                    yield
                    b = pbank()
                    for eh in range(2):
                        P.op("pe", lambda eh=eh, b=b: pe.matmul(PS[b][:, :], lhsT=ONES[:], rhs=SQ[:, eh, :], start=(eh == 0), stop=(eh == 1)),
                             reads=["ONES", "SQ%d" % eh], writes=["P%d" % b])
                    P.op("act", lambda b=b: act.activation(out=RR[:], in_=PS[b][:, :], func=AF.Ln, scale=1.0 / 256, bias=EPS), reads=["P%d" % b], writes=["RR"])
                    P.op("act", lambda: act.activation(out=RR[:], in_=RR[:], func=AF.Exp, scale=-0.5), reads=["RR"], writes=["RR"])
                    for eh in range(2):
                        fc = hh * 2 + eh
                        P.op("dve", lambda eh=eh, hh=hh: dve.scalar_tensor_tensor(out=YTMP[:, eh, :], in0=PS[4 + eh][:, :], scalar=GOUT[:, li * 2 + eh:li * 2 + eh + 1], in1=RR[:], op0=ALU.mult, op1=ALU.mult),
                             reads=["P%d" % (4 + eh), "GOUT", "RR"], writes=["YTMP%d" % eh])
                        ya, yk = yT_ap(fc)
                        P.op("dve", lambda eh=eh, fc=fc, ya=ya: dve.tensor_tensor(out=ya, in0=YTMP[:, eh, :], in1=ZS[:, fc, :], op=ALU.mult),
                             reads=["YTMP%d" % eh, "ZS%d" % fc], writes=[yk])
                    yield

            def conv_gen():
                for half in range(2):
                    cb0 = half * 4
                    P.op("dve", lambda cb0=cb0: dve.tensor_copy(out=U[:, :, 0:2], in_=UT[:, cb0:cb0 + 4, :]), reads=["UT"], writes=["U0", "U1", "U2", "U3"])

                    def ev_c(j, ps, pk):
                        P.op("act", lambda: act.copy(out=CS[:, j, :], in_=ps[:, :]), reads=[pk], writes=["CS%d" % j])

                    def ev_hc(j, ps, pk, cb0=cb0):
                        cbi = cb0 + j
                        P.op("dve", lambda: dve.tensor_tensor(out=U[:, j, 2:514], in0=ps[:, :], in1=CS[:, j, :], op=ALU.mult), reads=[pk, "CS%d" % j], writes=["U%d" % j])
                        P.op("dve", lambda: dve.tensor_copy(out=UT[:, cbi, :], in_=U[:, j, 512:514]), reads=["U%d" % j], writes=["UT"])
                        w0 = CW[:, li * 24 + cbi * 3 + 0:li * 24 + cbi * 3 + 1]
                        w1 = CW[:, li * 24 + cbi * 3 + 1:li * 24 + cbi * 3 + 2]
                        w2 = CW[:, li * 24 + cbi * 3 + 2:li * 24 + cbi * 3 + 3]
                        P.op("dve", lambda: dve.tensor_scalar(out=CS[:, j, :], in0=U[:, j, 0:512], scalar1=w0, scalar2=None, op0=ALU.mult), reads=["U%d" % j, "CW"], writes=["CS%d" % j])
                        P.op("dve", lambda: dve.scalar_tensor_tensor(out=CS[:, j, :], in0=U[:, j, 1:513], scalar=w1, in1=CS[:, j, :], op0=ALU.mult, op1=ALU.add), reads=["U%d" % j, "CW", "CS%d" % j], writes=["CS%d" % j])
                        P.op("dve", lambda: dve.scalar_tensor_tensor(out=CS[:, j, :], in0=U[:, j, 2:514], scalar=w2, in1=CS[:, j, :], op0=ALU.mult, op1=ALU.add), reads=["U%d" % j, "CW", "CS%d" % j], writes=["CS%d" % j])

                    def ev_b(j, ps, pk):
                        P.op("dve", lambda: dve.tensor_tensor(out=CS[:, j, :], in0=ps[:, :], in1=CS[:, j, :], op=ALU.mult), reads=[pk, "CS%d" % j], writes=["CS%d" % j])

                    def ev_zc(j, ps, pk, cb0=cb0):
                        e = j % 2
                        P.op("act", lambda: act.activation(out=T32[:, e, 0:512], in_=ps[:, :], func=AF.Silu), reads=[pk], writes=["T32_%d" % e])
                        ya, yk = yT_ap(8 + cb0 + j)
                        P.op("dve", lambda: dve.tensor_tensor(out=ya, in0=CS[:, j, :], in1=T32[:, e, 0:512], op=ALU.mult), reads=["CS%d" % j, "T32_%d" % e], writes=[yk])

                    for col0, ev in ((C0, ev_c), (HC0, ev_hc), (B0, ev_b), (ZC0, ev_zc)):
                        wi = wload(win_ap(li, col0 + half * 512, 512), 512)
                        for j in range(4):
                            b = pbank()
                            for kc in range(16):
                                ra, rk = xnT_ap(kc)
                                P.op("pe", lambda kc=kc, ra=ra, b=b, j=j, wi=wi: pe.matmul(PS[b][:, :], lhsT=W[wi][:, kc, j * 128:(j + 1) * 128], rhs=ra, start=(kc == 0), stop=(kc == 15)),
                                     reads=[rk, "W%d" % wi], writes=["P%d" % b])
                            ev(j, PS[b], "P%d" % b)
                            yield

            g1, g2 = gla_gen(), conv_gen()
            live1 = live2 = True
            while live1 or live2:
                if live1:
                    live1 = next(g1, "END") != "END"
                if live2:
                    live2 = next(g2, "END") != "END"

            S_flat = S[:].rearrange("p a b -> p (a b)")
            UT_flat = UT[:].rearrange("p a b -> p (a b)")
            if ring == 1:
                P.op("sp", lambda: sp.dma_start(out=st_loc[li, :, 0:1024], in_=S_flat), reads=["S0", "S1", "S2", "S3"], writes=["STL%d" % li], dma="stl%d" % li)
                P.op("sp", lambda: sp.dma_start(out=st_loc[li, :, 1024:SW], in_=UT_flat), reads=["UT"], writes=["STL%d" % li], dma="stl%d" % li)
            elif s < n_slots - 1:
                P.op("sp", lambda: sp.dma_start(out=st_out[:, 0:1024], in_=S_flat), reads=["S0", "S1", "S2", "S3"], writes=["STOUT"], dma="sto")
                P.op("sp", lambda: sp.dma_start(out=st_out[:, 1024:SW], in_=UT_flat), reads=["UT"], writes=["STOUT"], dma="sto")
                groups = [list(range(g * ring, (g + 1) * ring)) for g in range(n_ranks // ring)]
                P.op("pool", lambda: pool.collective_compute("AllGather", ALU.bypass, groups, [st_out[:, :]], [st_all[0:ring * 128, :]]),
                     reads=["STOUT"], writes=["STALL"], dma="cc")

            P.op("sp", lambda: sp.dma_start(out=GPOST, in_=bass.AP(gpost_d, li * D, [[0, 128], [1, D]])), writes=CSK, dma="gpost")
            for g in range(4):
                wi = wload(wout_ap(li, g), 512)
                for t in range(NSUB):
                    b = pbank()
                    for fc in range(16):
                        ya, yk = yT_ap(fc, t * 128, (t + 1) * 128)
                        P.op("pe", lambda fc=fc, ya=ya, b=b, wi=wi: pe.matmul(PS[b][:, :], lhsT=ya, rhs=W[wi][:, fc, 0:512], start=(fc == 0), stop=(fc == 15)),
                             reads=[yk, "W%d" % wi], writes=["P%d" % b])
                    ya2, yk2 = ysb_ap(t, g * 512, (g + 1) * 512)
                    P.op("dve", lambda b=b, ya2=ya2, g=g: dve.tensor_tensor(out=ya2, in0=PS[b][:, :], in1=GPOST[:, g * 512:(g + 1) * 512], op=ALU.mult), reads=["P%d" % b] + CSK, writes=yk2)
                    P.op("act", lambda b=b, t=t, g=g: act.activation(out=SQ[:, 0, :], in_=PS[b][:, :], func=AF.Square, accum_out=SSQ[:, 12 + t * 4 + g:13 + t * 4 + g]),
                         reads=["P%d" % b], writes=["SQ0", "SSQd%d" % t, "P%d" % b])
            for t in range(NSUB):
                P.op("dve", lambda t=t: dve.tensor_reduce(out=SSQ[:, 28 + t:29 + t], in_=SSQ[:, 12 + t * 4:16 + t * 4], axis=mybir.AxisListType.X, op=ALU.add),
                     reads=["SSQd%d" % t], writes=["SSQe%d" % t])
                P.op("act", lambda t=t: act.activation(out=SSQ[:, 28 + t:29 + t], in_=SSQ[:, 28 + t:29 + t], func=AF.Ln, scale=1.0 / D, bias=EPS), reads=["SSQe%d" % t], writes=["SSQe%d" % t])
                P.op("act", lambda t=t: act.activation(out=SSQ[:, 28 + t:29 + t], in_=SSQ[:, 28 + t:29 + t], func=AF.Exp, scale=-0.5), reads=["SSQe%d" % t], writes=["SSQe%d" % t])
                ya, yk = ysb_ap(t)
                P.op("dve", lambda t=t, ya=ya: dve.scalar_tensor_tensor(out=H[:, t, :], in0=ya, scalar=SSQ[:, 28 + t:29 + t], in1=H[:, t, :], op0=ALU.mult, op1=ALU.add),
                     reads=yk + ["SSQe%d" % t, "H%d" % t], writes=["H%d" % t])
                if li == DEPTH - 1 and s // DEPTH >= 1:
                    P.op("sp", lambda t=t: sp.dma_start(out=yout[s // DEPTH - 1, t * 128:(t + 1) * 128, :], in_=H[:, t, :]), reads=["H%d" % t], writes=["YOUT"], dma="yo%d" % t)

        P.wait_all("sp")
    return nc


def _consts():
    ident = np.eye(128, dtype=np.float32)
    jj, ii = np.meshgrid(np.arange(128), np.arange(128), indexing="ij")
    mask = (jj <= ii).astype(np.float32)
    tri = mask * np.float32(-1.0 / 16.0)
    ones = np.ones((128, 128), np.float32)
    return np.ascontiguousarray(np.concatenate([ident, mask, tri, ones], axis=1))


def _layout_params(norm_pre, w_gate_up, b_gate, gla_out_norm, conv_w, norm_post, rot):
    order = [(i - rot) % DEPTH for i in range(DEPTH)]
    gpre = np.concatenate([norm_pre[l].reshape(16, 128).T for l in order], axis=1)
    gout = np.concatenate([gla_out_norm[l].reshape(2, 128).T for l in order], axis=1)
    cw = np.concatenate([conv_w[l].reshape(3, 8, 128).transpose(2, 1, 0).reshape(128, 24) for l in order], axis=1)
    waug = np.stack([np.concatenate([w_gate_up[l], b_gate[l][None, :]], axis=0) for l in order], axis=0)
    gpost = np.stack([norm_post[l] for l in order], axis=0)
    f = lambda a: np.ascontiguousarray(a, dtype=np.float32)
    return f(gpre), f(gout), f(cw), f(waug), f(gpost), order


def run_ring1(x, meta_tokens, norm_pre, w_in, w_gate_up, b_gate, gla_out_norm, conv_w, w_out, norm_post, n_tok_tiles):
    x = np.asarray(x, np.float32)
    meta_tokens = np.asarray(meta_tokens, np.float32)
    B = x.shape[0]
    n_tiles = 1 + n_tok_tiles
    n_cores = B
    n_slots = n_tiles * DEPTH
    nc = build_program(n_slots, 1, n_cores, n_tiles)
    cst = _consts()
    gpre, gout, cw, waug, gpost, order = _layout_params(norm_pre, w_gate_up, b_gate, gla_out_norm, conv_w, norm_post, 0)
    w_in = np.asarray(w_in, np.float32)
    w_out = np.asarray(w_out, np.float32)
    in_maps = []
    for c in range(n_cores):
        xin = np.zeros((n_tiles, T, D), np.float32)
        xin[0, T - NMETA:, :] = meta_tokens
        xin[1:] = x[c].reshape(n_tok_tiles, T, D)
        keep = np.ones((128, 2 * n_slots), np.float32)
        oh = np.zeros((128, 8), np.float32)
        mp = dict(xin=xin, waug=waug, gpre=gpre, gpost=gpost, gout=gout, cw=cw, keep=keep, onehot=oh, cst=cst)
        for l in range(DEPTH):
            mp["w_in%d" % l] = np.ascontiguousarray(w_in[l])
            mp["w_out%d" % l] = np.ascontiguousarray(w_out[l])
        in_maps.append(mp)
    res = run_bass_kernel_spmd(nc, in_maps, core_ids=list(range(n_cores)))
    out = np.empty((B, n_tok_tiles * T, D), np.float32)
    for c in range(n_cores):
        out[c] = res.results[c]["yout"].reshape(n_tok_tiles * T, D)
    return out, res


def run_ring(x, meta_tokens, norm_pre, w_in, w_gate_up, b_gate, gla_out_norm, conv_w, w_out, norm_post, n_tok_tiles, ring=4):
    x = np.asarray(x, np.float32)
    meta_tokens = np.asarray(meta_tokens, np.float32)
    B = x.shape[0]
    n_tiles = 1 + n_tok_tiles
    n_cores = B * ring
    n_slots = n_tiles + ring - 1
    n_xt = (n_tiles + ring - 1) // ring
    nc = build_program(n_slots, ring, n_cores, n_xt)
    cst = _consts()
    w_in = np.asarray(w_in, np.float32)
    w_out = np.asarray(w_out, np.float32)
    per_rp = []
    for rp in range(ring):
        gpre, gout, cw, waug, gpost, order = _layout_params(norm_pre, w_gate_up, b_gate, gla_out_norm, conv_w, norm_post, rp)
        keep = np.zeros((128, 2 * n_slots), np.float32)
        for s in range(n_slots):
            m = s - rp
            if m >= 0 and m % DEPTH != 0:
                keep[:, s] = 1.0
            if m >= 0 and m % DEPTH == 0 and (DEPTH * 0 + ring * (m // DEPTH) + rp) < n_tiles:
                keep[:, n_slots + s] = 1.0
        oh = np.zeros((128, 8), np.float32)
        oh[:, (rp - 1) % ring] = 1.0
        mp = dict(waug=waug, gpre=gpre, gpost=gpost, gout=gout, cw=cw, keep=keep, onehot=oh, cst=cst)
        for l in range(DEPTH):
            mp["w_in%d" % l] = np.ascontiguousarray(w_in[order[l]])
            mp["w_out%d" % l] = np.ascontiguousarray(w_out[order[l]])
        per_rp.append(mp)
    in_maps = []
    for c in range(n_cores):
        b, rp = c // ring, c % ring
        xin = np.zeros((n_xt, T, D), np.float32)
        for i in range(n_xt):
            j = ring * i + rp
            if j == 0:
                xin[i, T - NMETA:, :] = meta_tokens
            elif j < n_tiles:
                xin[i] = x[b, (j - 1) * T:j * T, :]
        mp = dict(per_rp[rp])
        mp["xin"] = xin
        in_maps.append(mp)
    res = run_bass_kernel_spmd(nc, in_maps, core_ids=list(range(n_cores)))
    out = np.empty((B, n_tok_tiles * T, D), np.float32)
    for c in range(n_cores):
        b, rp = c // ring, c % ring
        yo = res.results[c]["yout"]
        for i in range(n_xt):
            j = ring * i + rp
            if 1 <= j < n_tiles:
                out[b, (j - 1) * T:j * T, :] = yo[DEPTH * i + rp]
    return out, res


def kernel(x, meta_tokens, norm_pre, w_in, w_gate_up, b_gate, gla_out_norm, conv_w, w_out, norm_post):
    out, _ = run_ring1(x, meta_tokens, norm_pre, w_in, w_gate_up, b_gate, gla_out_norm, conv_w, w_out, norm_post, SEQ // T)
    return out
```
